# Optimizing a Trainium2 kernel written in Bass

```python
import jax, jax.numpy as jnp
from jax import lax
import numpy as np

D_MODEL = 1024
BATCH = 16
SEQ = 4096
DEPTH = 4

HEAD_DIM = 64
ROPE_THETA = 500000.0
ROT_DIM = HEAD_DIM // 4
NORM_EPS = 1e-6
NEG_INF = -1e30
D_FF = 4 * D_MODEL

A_WIDTH = D_MODEL // 2
A_HEADS = A_WIDTH // HEAD_DIM
A_KV_HEADS = A_HEADS // 4
A_GROUP = A_HEADS // A_KV_HEADS
A_KV_WIDTH = A_KV_HEADS * HEAD_DIM
A_RADIUS = 128
POOL_WIDTH = D_MODEL - A_WIDTH
POOL_WINDOWS = (2, 4, 8, 16)
POOL_GROUPS = len(POOL_WINDOWS)
POOL_GROUP_WIDTH = POOL_WIDTH // POOL_GROUPS
EVEN_IN = A_WIDTH + 2 * A_KV_WIDTH + POOL_WIDTH
EVEN_MIX = A_WIDTH + POOL_WIDTH

C_PATTERNS = ((128, 1), (512, 4), (2048, 16))
N_C_GROUPS = len(C_PATTERNS)
C_HEADS_PER_GROUP = D_MODEL // (2 * HEAD_DIM)
C_KV_HEADS = C_HEADS_PER_GROUP // 2
C_GROUP = C_HEADS_PER_GROUP // C_KV_HEADS
C_Q_WIDTH = N_C_GROUPS * C_HEADS_PER_GROUP * HEAD_DIM
C_KV_WIDTH = C_KV_HEADS * HEAD_DIM
ODD_IN = C_Q_WIDTH + 2 * C_KV_WIDTH
C_MIX = C_HEADS_PER_GROUP * HEAD_DIM

N_EVEN = (DEPTH + 1) // 2
N_ODD = DEPTH // 2

kernel_name = "hybrid_bidir_swa_pool_dilated"


def rms_norm(x, g):
    xf = x.astype(jnp.float32)
    y = xf * lax.rsqrt(jnp.mean(xf * xf, axis=-1, keepdims=True) + NORM_EPS)
    return (y * g.astype(jnp.float32)).astype(x.dtype)


def rope_tables(seq_len):
    pos = jnp.arange(seq_len, dtype=jnp.float32)
    inv = ROPE_THETA ** (-jnp.arange(0, ROT_DIM, 2, dtype=jnp.float32) / ROT_DIM)
    ang = pos[:, None] * inv[None, :]
    return jnp.cos(ang), jnp.sin(ang)


def apply_partial_rope(x, cos, sin):
    half = ROT_DIM // 2
    xr = x[..., :ROT_DIM].astype(jnp.float32)
    x1, x2 = xr[..., :half], xr[..., half:]
    c, s = cos[None, :, None, :], sin[None, :, None, :]
    rot = jnp.concatenate([x1 * c - x2 * s, x2 * c + x1 * s], axis=-1).astype(x.dtype)
    return jnp.concatenate([rot, x[..., ROT_DIM:]], axis=-1)


def banded_attention(q, k, v, radius, sink=None):
    n, length, kvh, grp, hd = q.shape
    blk = radius
    nb = -(-length // blk)
    pad = nb * blk - length
    qb = jnp.pad(q, ((0, 0), (0, pad), (0, 0), (0, 0), (0, 0))).reshape(n, nb, blk, kvh, grp, hd)
    kv_pad = ((0, 0), (blk, blk + pad), (0, 0), (0, 0))
    kb = jnp.pad(k, kv_pad).reshape(n, nb + 2, blk, kvh, hd)
    vb = jnp.pad(v, kv_pad).reshape(n, nb + 2, blk, kvh, hd)
    kw = jnp.concatenate([kb[:, :-2], kb[:, 1:-1], kb[:, 2:]], axis=2)
    vw = jnp.concatenate([vb[:, :-2], vb[:, 1:-1], vb[:, 2:]], axis=2)
    s = jnp.einsum('nbqhgd,nbkhd->nbhgqk', qb, kw,
                   preferred_element_type=jnp.float32) * (hd ** -0.5)
    q_pos = jnp.arange(nb * blk).reshape(nb, blk)
    k_pos = jnp.arange(nb)[:, None] * blk - blk + jnp.arange(3 * blk)[None, :]
    kp = k_pos[:, None, :]
    valid = (jnp.abs(q_pos[:, :, None] - kp) <= radius) & (kp >= 0) & (kp < length)
    s = jnp.where(valid[None, :, None, None], s, NEG_INF)
    m = jnp.max(s, axis=-1)
    if sink is not None:
        sk = sink.astype(jnp.float32)[None, None, :, :, None]
        m = jnp.maximum(m, sk)
    p = jnp.exp(s - m[..., None])
    denom = jnp.sum(p, axis=-1)
    if sink is not None:
        denom = denom + jnp.exp(sk - m)
    o = jnp.einsum('nbhgqk,nbkhd->nbhgqd', p, vw.astype(jnp.float32)) / denom[..., None]
    lse = m + jnp.log(denom)
    o = o.transpose(0, 1, 4, 2, 3, 5).reshape(n, nb * blk, kvh, grp, hd)[:, :length]
    lse = lse.transpose(0, 1, 4, 2, 3).reshape(n, nb * blk, kvh, grp)[:, :length]
    return o.astype(q.dtype), lse


def pool_mixer(p, w_pool, scale):
    b, s, _ = p.shape
    pf = p.astype(jnp.float32)
    cs = jnp.pad(jnp.cumsum(pf, axis=1), ((0, 0), (1, 0), (0, 0)))
    t = jnp.arange(s)
    outs = []
    for g, w in enumerate(POOL_WINDOWS):
        lo = jnp.clip(t - w // 2, 0, s)
        hi = jnp.clip(t + w // 2, 0, s)
        sl = slice(g * POOL_GROUP_WIDTH, (g + 1) * POOL_GROUP_WIDTH)
        csg = cs[:, :, sl]
        mean = (csg[:, hi] - csg[:, lo]) / (hi - lo).astype(jnp.float32)[None, :, None]
        outs.append(jnp.einsum('bsc,cd->bsd', (mean - pf[..., sl]).astype(p.dtype), w_pool[g]))
    return jnp.concatenate(outs, axis=-1) * scale


def even_mixer(h, w_in, w_out, sink, w_pool, pool_scale, cos, sin):
    b, s, _ = h.shape
    proj = h @ w_in
    q, k, v, p = jnp.split(proj, [A_WIDTH, A_WIDTH + A_KV_WIDTH, A_WIDTH + 2 * A_KV_WIDTH], axis=-1)
    q = apply_partial_rope(q.reshape(b, s, A_HEADS, HEAD_DIM), cos, sin)
    k = apply_partial_rope(k.reshape(b, s, A_KV_HEADS, HEAD_DIM), cos, sin)
    v = v.reshape(b, s, A_KV_HEADS, HEAD_DIM)
    q = q.reshape(b, s, A_KV_HEADS, A_GROUP, HEAD_DIM)
    attn, _ = banded_attention(q, k, v, A_RADIUS, sink.reshape(A_KV_HEADS, A_GROUP))
    pool = pool_mixer(p, w_pool, pool_scale)
    return jnp.concatenate([attn.reshape(b, s, A_WIDTH), pool], axis=-1) @ w_out


def to_strided(x, d):
    b, s = x.shape[:2]
    rest = x.shape[2:]
    x = jnp.moveaxis(x.reshape(b, s // d, d, *rest), 2, 1)
    return x.reshape(b * d, s // d, *rest)


def from_strided(x, b, d):
    n, l = x.shape[:2]
    rest = x.shape[2:]
    x = jnp.moveaxis(x.reshape(b, d, l, *rest), 1, 2)
    return x.reshape(b, l * d, *rest)


def odd_mixer(h, w_in, w_out, cos, sin):
    b, s, _ = h.shape
    proj = h @ w_in
    q, k, v = jnp.split(proj, [C_Q_WIDTH, C_Q_WIDTH + C_KV_WIDTH], axis=-1)
    q = apply_partial_rope(q.reshape(b, s, N_C_GROUPS * C_HEADS_PER_GROUP, HEAD_DIM), cos, sin)
    q = q.reshape(b, s, N_C_GROUPS, C_KV_HEADS, C_GROUP, HEAD_DIM)
    k = apply_partial_rope(k.reshape(b, s, C_KV_HEADS, HEAD_DIM), cos, sin)
    v = v.reshape(b, s, C_KV_HEADS, HEAD_DIM)
    outs, lses = [], []
    for g, (window, dil) in enumerate(C_PATTERNS):
        radius = window // (2 * dil)
        o, lse = banded_attention(to_strided(q[:, :, g], dil), to_strided(k, dil),
                                  to_strided(v, dil), radius)
        outs.append(from_strided(o, b, dil))
        lses.append(from_strided(lse, b, dil))
    o = jnp.stack(outs, axis=0)
    wts = jax.nn.softmax(jnp.stack(lses, axis=0), axis=0)
    mixed = jnp.sum(wts[..., None] * o.astype(jnp.float32), axis=0).astype(h.dtype)
    return mixed.reshape(b, s, C_MIX) @ w_out


def setup_inputs(seed: int = 0) -> dict:
    key = jax.random.key(seed)
    ks = jax.random.split(key, 14)
    nrm = jax.random.normal
    f32 = jnp.float32
    return {
        "x": nrm(ks[0], (BATCH, SEQ, D_MODEL), f32),
        "norm_mix": 1.0 + 0.02 * nrm(ks[1], (DEPTH, D_MODEL), f32),
        "norm_mlp": 1.0 + 0.02 * nrm(ks[2], (DEPTH, D_MODEL), f32),
        "norm_final": 1.0 + 0.02 * nrm(ks[3], (D_MODEL,), f32),
        "w_in_even": nrm(ks[4], (N_EVEN, D_MODEL, EVEN_IN), f32) * D_MODEL ** -0.5,
        "w_out_even": nrm(ks[5], (N_EVEN, EVEN_MIX, D_MODEL), f32) * EVEN_MIX ** -0.5,
        "sink_logits": 0.5 * nrm(ks[6], (N_EVEN, A_HEADS), f32),
        "w_pool": nrm(ks[7], (N_EVEN, POOL_GROUPS, POOL_GROUP_WIDTH, POOL_GROUP_WIDTH), f32) * POOL_GROUP_WIDTH ** -0.5,
        "pool_scale": 1.0 + 0.1 * nrm(ks[8], (N_EVEN, POOL_WIDTH), f32),
        "w_in_odd": nrm(ks[9], (N_ODD, D_MODEL, ODD_IN), f32) * D_MODEL ** -0.5,
        "w_out_odd": nrm(ks[10], (N_ODD, C_MIX, D_MODEL), f32) * C_MIX ** -0.5,
        "w_up": nrm(ks[11], (DEPTH, D_MODEL, D_FF), f32) * D_MODEL ** -0.5,
        "w_down": nrm(ks[12], (DEPTH, D_FF, D_MODEL), f32) * D_FF ** -0.5,
    }


def reference(x, norm_mix, norm_mlp, norm_final, w_in_even, w_out_even, sink_logits,
              w_pool, pool_scale, w_in_odd, w_out_odd, w_up, w_down):
    cos, sin = rope_tables(x.shape[1])
    for layer in range(DEPTH):
        i = layer // 2
        h = rms_norm(x, norm_mix[layer])
        if layer % 2 == 0:
            mix = even_mixer(h, w_in_even[i], w_out_even[i], sink_logits[i],
                             w_pool[i], pool_scale[i], cos, sin)
        else:
            mix = odd_mixer(h, w_in_odd[i], w_out_odd[i], cos, sin)
        x = x + mix
        h = rms_norm(x, norm_mlp[layer])
        x = x + jnp.square(jax.nn.relu(h @ w_up[layer])) @ w_down[layer]
    return rms_norm(x, norm_final)
```

```python
import numpy as np
import concourse.bass as bass
import concourse.mybir as mybir
from concourse.bass_utils import run_bass_kernel_spmd

F32 = mybir.dt.float32
BF16 = mybir.dt.bfloat16
ALU = mybir.AluOpType
AF = mybir.ActivationFunctionType

D = 1024
S = 4096
NSEQ = 2
DEPTH = 4
DFF = 4096
HD = 64
EPS = 1e-6
TT = 512
NT = S // TT
DILS = (1, 4, 16)
VT_MAX = 48
N_DMA_SEMS = 24
RECIP_POOL = False
ODD_PIPE = True
USE_RSQRT = False
USE_LNEXP = True
ODD_PEMASK = True
PERM_ENG = 'pool'
EVEN_COLS = 14 * 128 + 128
ODD_COLS = 3 * 1024 + 512 + 256


class Op:
    __slots__ = ("eng", "fn", "deps", "flag", "sem", "val", "snap", "is_dma", "idx", "noself")

    def __init__(self, eng, fn, is_dma):
        self.eng = eng
        self.fn = fn
        self.deps = ()
        self.flag = False
        self.sem = None
        self.val = 0
        self.snap = None
        self.is_dma = is_dma
        self.idx = 0
        self.noself = False


class Prog:
    ENG = ("pe", "act", "dve", "pool", "sp")

    def __init__(self, nc, eng_sems, dma_sems, same_engine_sync=True):
        self.nc = nc
        self.ops = []
        self.state = {}
        self.ses = same_engine_sync
        self.eng_sems = eng_sems
        self.dma_sems = dma_sems
        self.emitted = 0
        self.known = {e: {} for e in self.ENG}
        self.counts = {e: 0 for e in self.ENG}
        self.dma_hist = [None] * len(dma_sems)
        self.dma_vals = [0] * len(dma_sems)
        self.dma_n = 0
        self.last_op = {}
        self.dma_since = []
        self.pending = {}
        self.nbank = 0

    def bank(self):
        b = self.nbank % 8
        self.nbank += 1
        return b

    def add(self, eng, fn, reads=(), writes=(), is_dma=False, noself=False):
        op = Op(eng, fn, is_dma)
        op.noself = noself
        op.idx = len(self.ops)
        deps = {}
        st = self.state
        for k in reads:
            s = st.get(k)
            if s is not None:
                for w in s[0]:
                    deps[w.idx] = w
        for k in writes:
            s = st.get(k)
            if s is not None:
                for w in s[0]:
                    deps[w.idx] = w
                for r in s[1]:
                    deps[r.idx] = r
        pend = self.pending.pop(eng, None)
        if pend:
            for d in pend:
                deps[d.idx] = d
        op.deps = list(deps.values())
        for k in writes:
            st[k] = [[op], []]
        for k in reads:
            s = st.get(k)
            if s is None:
                s = st[k] = [[], []]
            rl = s[1]
            if not is_dma:
                rl[:] = [r for r in rl if r.is_dma or r.eng != eng]
            rl.append(op)
        self.ops.append(op)
        if is_dma:
            self.dma_since.append(op)
        else:
            self.last_op[eng] = op
        return op

    def barrier(self):
        deps = list(self.last_op.values()) + self.dma_since
        for d_ in deps:
            d_.flag = True
        self.dma_since = []
        for e in self.ENG:
            self.pending[e] = list(deps) + self.pending.get(e, [])
        self.state = {}

    def emit(self):
        ses = self.ses
        new = self.ops[self.emitted:]
        for op in new:
            for d in op.deps:
                if d.is_dma or op.is_dma or d.eng != op.eng:
                    d.flag = True
                elif ses and op.eng != "pe" and not op.noself:
                    d.flag = True
        streams = {e: [] for e in self.ENG}
        nd = len(self.dma_sems)
        for op in new:
            e = op.eng
            kn = self.known[e]
            st = streams[e]
            for d in op.deps:
                if d.sem is None:
                    if d.idx < self.emitted:
                        raise RuntimeError("dependency on unflagged emitted op")
                    continue
                if (not d.is_dma) and (not op.is_dma) and d.eng == e and (e == "pe" or not ses or op.noself):
                    continue
                sid = id(d.sem)
                if kn.get(sid, 0) >= d.val:
                    continue
                st.append(("wait", d.sem, d.val))
                for k2, v2 in d.snap.items():
                    if kn.get(k2, 0) < v2:
                        kn[k2] = v2
                kn[sid] = d.val
            if op.is_dma:
                slot = self.dma_n % nd
                self.dma_n += 1
                prev = self.dma_hist[slot]
                sem = self.dma_sems[slot]
                if prev is not None and kn.get(id(sem), 0) < prev.val:
                    st.append(("wait", sem, prev.val))
                    kn[id(sem)] = prev.val
                self.dma_vals[slot] += 16
                op.sem = sem
                op.val = self.dma_vals[slot]
                self.dma_hist[slot] = op
                op.snap = dict(kn)
                st.append(("dma", op.fn, sem))
            else:
                if op.flag:
                    sem = self.eng_sems[e]
                    self.counts[e] += 1
                    op.sem = sem
                    op.val = self.counts[e]
                    op.snap = dict(kn)
                    op.snap[id(sem)] = op.val
                    st.append(("op", op.fn, sem))
                else:
                    st.append(("op", op.fn, None))
        self.emitted = len(self.ops)
        return streams


def replay(engine, items):
    for it in items:
        if it[0] == "wait":
            engine.wait_ge(it[1], it[2])
        elif it[0] == "dma":
            it[1](engine).then_inc(it[2], 16)
        else:
            ins = it[1](engine)
            if it[2] is not None:
                ins.then_inc(it[2], 1)


def bc(ap_tensor, offset, dims):
    return bass.AP(ap_tensor, offset, dims)


class Builder:
    def __init__(self, n_layers=DEPTH, nseq=NSEQ, debug_dump=None):
        self.n_layers = n_layers
        self.nseq = nseq
        self.debug_dump = debug_dump
        self.nc = bass.Bass("TRN2", target_bir_lowering=False)

    def uniq(self, name):
        self._u = getattr(self, '_u', 0) + 1
        return '%s_%d' % (name, self._u)

    def mm(self, out, lhsT, rhs, start, stop, reads, writes):
        self.P.add("pe", lambda e, o=out, l=lhsT, r=rhs, a=start, b=stop: e.matmul(o, l, r, start=a, stop=b),
                   reads, writes)

    def act(self, out, in_, func, reads, writes, bias=None, scale=None):
        kw = {}
        if bias is not None:
            kw["bias"] = bias
        if scale is not None:
            kw["scale"] = scale
        self.P.add("act", lambda e, o=out, i=in_, f=func, k=kw: e.activation(o, i, f, **k), reads, writes)

    def tt(self, eng, out, in0, in1, op, reads, writes, noself=False):
        self.P.add(eng, lambda e, o=out, a=in0, b=in1, p=op: e.tensor_tensor(o, a, b, p), reads, writes,
                   noself=noself)

    def stt(self, eng, out, in0, scalar, in1, op0, op1, reads, writes):
        self.P.add(eng, lambda e, o=out, a=in0, s=scalar, b=in1, p0=op0, p1=op1:
                   e.scalar_tensor_tensor(o, a, s, b, p0, p1), reads, writes)

    def copy(self, eng, out, in_, reads, writes, noself=False):
        if eng == "act":
            self.P.add("act", lambda e, o=out, i=in_: e.copy(o, i), reads, writes, noself=noself)
        else:
            self.P.add(eng, lambda e, o=out, i=in_: e.tensor_copy(o, i), reads, writes, noself=noself)

    def recip(self, out, in_, ones, reads, writes):
        if USE_LNEXP:
            self.P.add("act", lambda e, o=out, i=in_: e.activation(o, i, AF.Ln), reads, writes)
            self.P.add("act", lambda e, o=out: e.activation(o, o, AF.Exp, scale=-1.0), writes, writes)
        elif RECIP_POOL:
            self.P.add("pool", lambda e, o=out, i=in_, on=ones: e.tensor_tensor(o, i, on, ALU.pow), reads, writes)
        else:
            self.P.add("dve", lambda e, o=out, i=in_: e.reciprocal(o, i), reads, writes)

    def memset(self, eng, ap, val, writes):
        self.P.add(eng, lambda e, a=ap, v=val: e.memset(a, v), (), writes)

    def dma(self, out, in_, reads, writes):
        self.P.add("sp", lambda e, o=out, i=in_: e.dma_start(out=o, in_=i), reads, writes, is_dma=True)

    def flush(self, final=False):
        self.P.barrier()
        streams = self.P.emit()
        if final:
            sp = streams["sp"]
            for slot, h in enumerate(self.P.dma_hist):
                if h is not None:
                    sp.append(("wait", h.sem, h.val))
        with self.nc.Block() as block:
            @block.tensor
            def _(e):
                replay(e, streams["pe"])

            @block.scalar
            def _(e):
                replay(e, streams["act"])

            @block.vector
            def _(e):
                replay(e, streams["dve"])

            @block.gpsimd
            def _(e):
                replay(e, streams["pool"])

            @block.sync
            def _(e):
                replay(e, streams["sp"])

    def build(self):
        nc = self.nc
        ns = self.nseq
        dt = nc.dram_tensor
        self.xin = dt("xT", [ns, D, S], F32, kind="ExternalInput").ap()
        self.prm_d = dt("prm", [128, 96], F32, kind="ExternalInput").ap()
        self.rope_d = dt("rope", [3, 128, 2, S], F32, kind="ExternalInput").ap()
        self.mask_d = dt("masks", [128, 6, 128], F32, kind="ExternalInput").ap()
        self.wine_d = dt("w_in_e", [2, D, EVEN_COLS], F32, kind="ExternalInput").ap()
        self.woute_d = dt("w_out_e", [2, D, D], F32, kind="ExternalInput").ap()
        self.wpool_d = dt("w_pool", [2, 4, 128, 128], F32, kind="ExternalInput").ap()
        self.wino_d = dt("w_in_o", [2, D, ODD_COLS], F32, kind="ExternalInput").ap()
        self.wouto_d = dt("w_out_o", [2, 512, D], F32, kind="ExternalInput").ap()
        self.wup_d = dt("w_up", [DEPTH, D, DFF], F32, kind="ExternalInput").ap()
        self.wdn_d = dt("w_down", [DEPTH, DFF, D], F32, kind="ExternalInput").ap()
        self.yout = dt("yT", [ns, D, S], F32, kind="ExternalOutput").ap()
        self.xs = dt("xs", [ns, D, S], F32, kind="Internal").ap()
        self.qs = dt("qs", [ns, 12, 128, S], BF16, kind="Internal").ap()
        self.ks = dt("ks", [ns, 3, 2, 128, S + 64], BF16, kind="Internal").ap()
        self.vs = dt("vs", [ns, 3, 128, VT_MAX, 256], BF16, kind="Internal").ap()
        self.pscr = dt("pscr", [ns, 4, 128, S + 16], F32, kind="Internal").ap()
        self.mix = dt("mix", [ns, 8, 128, S], BF16, kind="Internal").ap()
        self.hs = dt("hs", [ns, 128, 8, S], BF16, kind="Internal").ap()
        self.wus = dt("wus", [DEPTH, 8, 128, 8 * 512], BF16, kind="Internal").ap()
        self.wds = dt("wds", [DEPTH, 8, 128, 32 * 128], BF16, kind="Internal").ap()
        if self.debug_dump:
            self.dbg = dt("dbg", [ns, D, S], F32, kind="ExternalOutput").ap()

        import contextlib
        with contextlib.ExitStack() as top:
            esems = {e: top.enter_context(nc.semaphore("sem_" + e)) for e in Prog.ENG}
            dsems = [top.enter_context(nc.semaphore("dsem%d" % i)) for i in range(N_DMA_SEMS)]
            self.P = Prog(nc, esems, dsems)
            sb = lambda name, shape, d: top.enter_context(nc.sbuf_tensor(name, shape, d))
            self.psum = top.enter_context(nc.psum_tensor("psum", [128, 8, 512], F32))
            self.prm = sb("prm_sb", [128, 96], F32)
            self.ones = sb("ones", [128, 128], BF16)
            self.epsb = sb("epsb", [128, 1], F32)
            self.onesf = sb("onesf", [128, TT], F32)
            self.maskf = sb("maskf", [128, 6, 128], F32)
            self.ident = sb("ident", [128, 128], BF16)
            self.permm = sb("permm", [128, 128], BF16)
            self.negb = sb("negb", [128, 4, 128], BF16)
            self.neg3 = sb("neg3", [128, 3, 512], BF16)
            self.negE = sb("negE", [128, 2, 512], BF16)
            self.mask3 = sb("mask3", [128, 3, 512], BF16)
            self.esink = sb("esink", [128, 2, 16], F32)
            self.esinkT = sb("esinkT", [128, 2, 2, 512], F32)
            with nc.allow_low_precision("bf16 matmul operands by design"):
                self.setup()
                self.flush()
                self.bg = None
                self.cast_at = {}
                for l in range(self.n_layers):
                    if l % 2 == 1 or l == 0 or ns < 2:
                        self.cast_at[(l, 0)] = l
                    else:
                        self.cast_at[(l - 1, ns - 1)] = l
                for l in range(self.n_layers):
                    for s in range(ns):
                        self.phase1(l, s)
                        self.P.barrier()
                        self.flush()
                        if l % 2 == 0:
                            self.attn_even(l, s)
                        else:
                            self.attn_odd(l, s)
                        self.P.barrier()
                        self.flush()
                        self.phase34(l, s)
                        self.P.barrier()
                        self.flush()
                self.P.barrier()
                self.memset("dve", self.epsb[:, 0:1], EPS, [("eps",)])
                self.flush(final=True)
        return nc

    def setup(self):
        P = self.P
        self.dma(self.prm[:, :], self.prm_d[:, :], [], [("prm",)])
        self.dma(self.maskf[:, :, :], self.mask_d[:, :, :], [], [("maskf",)])
        self.memset("dve", self.ones[:, :], 1.0, [("ones",)])
        self.memset("dve", self.epsb[:, :], EPS, [("eps",)])
        self.memset("dve", self.onesf[:, :], -1.0 if RECIP_POOL else 1.0, [("onesf",)])
        self.copy("dve", self.ident[:, :], self.maskf[:, 4, :], [("maskf",)], [("ident",)])
        self.copy("dve", self.permm[:, :], self.maskf[:, 5, :], [("maskf",)], [("permm",)])
        self.P.add("dve", lambda e: e.tensor_scalar(self.negb[:, :, :], self.maskf[:, 0:4, :], -1.0, 30000.0, ALU.add,
                                                    ALU.mult), [("maskf",)], [("negb",)])
        combos = ((0, 1), (2, 1), (0, 3))
        for ci, (ml, mr) in enumerate(combos):
            for t, mi in enumerate((ml, mr)):
                for h in range(2):
                    self.copy("dve", self.neg3[:, ci, (t * 2 + h) * 128:(t * 2 + h + 1) * 128],
                              self.negb[:, mi, :], [("negb",)], [("neg3", ci, t, h)])
                    self.copy("dve", self.mask3[:, ci, (t * 2 + h) * 128:(t * 2 + h + 1) * 128],
                              self.maskf[:, mi, :], [("maskf",)], [("mask3", ci, t, h)])
        for mi in range(2):
            for h in range(4):
                self.copy("dve", self.negE[:, mi, h * 128:(h + 1) * 128], self.negb[:, mi, :], [("negb",)],
                          [("negE", mi, h)])
        self.act(self.esink[:, :, 0:8], self.prm[:, 80:96].rearrange("p (i h) -> p i h", i=2), AF.Exp,
                 [("prm",)], [("esink",)])
        for i in range(2):
            for kv in range(2):
                for h in range(4):
                    dst = self.esinkT[:, i, kv, h * 128:(h + 1) * 128]
                    self.memset("dve", dst, 0.0, [("esT", i, kv, h)])
                    hh = kv * 4 + h
                    self.P.add("dve", lambda e, o=dst, s=self.esink[:, i, hh:hh + 1]:
                               e.tensor_scalar(o, o, s, None, ALU.add), [("esink",), ("esT", i, kv, h)],
                               [("esT", i, kv, h)])

    def cast_steps(self, l, wf, wb):
        groups = []
        for kind in ("up", "down"):
            for g in range(8):
                for hf in range(2):
                    groups.append((kind, g, hf))

        def aps(n):
            kind, g, hf = groups[n]
            b = n % 2
            if kind == "up":
                src = self.wup_d[l].rearrange("(kc p) f -> p kc f", p=128)[:, hf * 4:(hf + 1) * 4,
                                                                         g * 512:(g + 1) * 512]
                dstv = wf[b][:, :].rearrange("p (kc f) -> p kc f", kc=4)
                dst_d = self.wus[l, g][:, hf * 2048:(hf + 1) * 2048]
            else:
                src = self.wdn_d[l].rearrange("(f p) d -> p f d", p=128)[:, hf * 16:(hf + 1) * 16,
                                                                         g * 128:(g + 1) * 128]
                dstv = wf[b][:, :].rearrange("p (f d) -> p f d", f=16)
                dst_d = self.wds[l, g][:, hf * 2048:(hf + 1) * 2048]
            return b, src, dstv, dst_d

        b, src, dstv, dst_d = aps(0)
        self.dma(dstv, src, [], [("cwf", b)])
        yield
        for n in range(len(groups)):
            b, src, dstv, dst_d = aps(n)
            if n + 1 < len(groups):
                b2, src2, dstv2, _ = aps(n + 1)
                self.dma(dstv2, src2, [], [("cwf", b2)])
            yield
            for q in range(2):
                sl = slice(q * 1024, (q + 1) * 1024)
                self.copy("pool" if (q + n) % 2 == 0 else "act", wb[b][:, sl], wf[b][:, sl], [("cwf", b)],
                          [("cwb", b, q)])
                yield
            self.dma(dst_d, wb[b][:, :], [("cwb", b, q) for q in range(2)], [])
            yield

    def bg_begin(self, l, st):
        nc = self.nc
        wf = [st.enter_context(nc.sbuf_tensor(self.uniq("cwf%d" % i), [128, 2048], F32)) for i in range(2)]
        wb = [st.enter_context(nc.sbuf_tensor(self.uniq("cwb%d" % i), [128, 2048], BF16)) for i in range(2)]
        self.bg = self.cast_steps(l, wf, wb)

    def bg_step(self, k=1):
        if self.bg is None:
            return
        for _ in range(k):
            try:
                next(self.bg)
            except StopIteration:
                self.bg = None
                return

    def bg_finish(self):
        while self.bg is not None:
            self.bg_step(1)

    def norm_tile(self, xt, gcol, hout, tag, sq, rt, rstd, xkey, hkey):
        bk = self.P.bank()
        ps = self.psum[:, bk, :]
        for c in range(8):
            self.act(sq[:, c, :], xt[:, c, :], AF.Square, [xkey(c)], [(tag, "sq", c)])
        for c in range(8):
            self.mm(ps, self.ones[:, :], sq[:, c, :], c == 0, c == 7, [(tag, "sq", c), ("ones",)], [("bank", bk)])
        if USE_LNEXP:
            self.act(rt[:, :], ps, AF.Ln, [("bank", bk), ("eps",)], [(tag, "rt")], bias=self.epsb[:, 0:1],
                     scale=1.0 / D)
            self.act(rstd[:, :], rt[:, :], AF.Exp, [(tag, "rt")], [(tag, "rstd")], scale=-0.5)
        elif USE_RSQRT:
            self.act(rstd[:, :], ps, AF.Abs_reciprocal_sqrt, [("bank", bk), ("eps",)], [(tag, "rstd")],
                     bias=self.epsb[:, 0:1], scale=1.0 / D)
        else:
            self.act(rt[:, :], ps, AF.Sqrt, [("bank", bk), ("eps",)], [(tag, "rt")], bias=self.epsb[:, 0:1],
                     scale=1.0 / D)
            self.recip(rstd[:, :], rt[:, :], self.onesf[:, :], [(tag, "rt"), ("onesf",)], [(tag, "rstd")])
        for c in range(8):
            self.stt("dve", hout(c), xt[:, c, :], self.prm[:, gcol + c:gcol + c + 1], rstd[:, :], ALU.mult, ALU.mult,
                     [xkey(c), (tag, "rstd"), ("prm",)], [hkey(c)])

    def phase1(self, l, s):
        nc = self.nc
        even = (l % 2 == 0)
        li = l // 2
        src = self.xin if l == 0 else self.xs
        import contextlib
        with contextlib.ExitStack() as st:
            sb = lambda name, shape, d: st.enter_context(nc.sbuf_tensor(self.uniq(name), shape, d))
            hT = sb("hT", [128, 8, S], BF16)
            if l == 0:
                with contextlib.ExitStack() as st2:
                    sb2 = lambda name, shape, d: st2.enter_context(nc.sbuf_tensor(self.uniq(name), shape, d))
                    xb = [sb2("p1x%d" % i, [128, 8, TT], F32) for i in range(2)]
                    sq = sb2("p1sq", [128, 8, TT], BF16)
                    rt = sb2("p1rt", [128, TT], F32)
                    rstd = sb2("p1rstd", [128, TT], F32)
                    for j in range(NT):
                        xt = xb[j % 2]
                        xtag = ("n1x", j % 2)
                        self.dma(xt[:, :, :], src[s].rearrange("(c p) t -> p c t", p=128)[:, :, j * TT:(j + 1) * TT],
                                 [("xres", s, j)], [(xtag, "x", c) for c in range(8)])
                        self.norm_tile(xt, l * 8, lambda c, j=j: hT[:, c, j * TT:(j + 1) * TT], ("n1",), sq, rt, rstd,
                                       lambda c, xtag=xtag: (xtag, "x", c), lambda c, j=j: ("hT", c, j))
                    self.flush()
            else:
                for j in range(NT):
                    self.dma(hT[:, :, j * TT:(j + 1) * TT], self.hs[s][:, :, j * TT:(j + 1) * TT], [],
                             [("hT", c, j) for c in range(8)])
            wf = [sb("p1wf%d" % i, [128, 8, 256], F32) for i in range(2)]
            wb = [sb("p1wb%d" % i, [128, 8, 256], BF16) for i in range(2)]
            ropeT = sb("p1ropeT", [128, 2, S], F32)
            for j in range(NT):
                self.dma(ropeT[:, :, j * TT:(j + 1) * TT], self.rope_d[0][:, :, j * TT:(j + 1) * TT], [], [("ropeT", j)])
            NB_ = 4
            t1b = [sb("p1t1_%d" % i, [128, TT], F32) for i in range(NB_)]
            t2b = [sb("p1t2_%d" % i, [128, TT], F32) for i in range(NB_)]
            qtmp = [sb("p1qt%d" % i, [128, TT], BF16) for i in range(NB_)]
            qstg = [sb("p1qs%d" % i, [128, S + 64], BF16) for i in range(2 if even else 4)]
            self.ropen = 0
            self.addn = 0
            vstg = sb("p1vs", [128, 32 if even else VT_MAX, 128], BF16)
            self.cnt = getattr(self, "cnt", 0)
            hkeys = [("hT", c, j) for c in range(8) for j in range(NT)]

            def load_w(wsrc, col0, ncols, slot):
                self.dma(wf[slot][:, :, 0:ncols], wsrc.rearrange("(kc p) f -> p kc f", p=128)[:, :, col0:col0 + ncols],
                         [], [("wf", slot)])
                self.copy("act", wb[slot][:, :, 0:ncols], wf[slot][:, :, 0:ncols], [("wf", slot)], [("wb", slot)])

            def tok_ap(c, j, d):
                if d == 1:
                    return hT[:, c, j * TT:(j + 1) * TT]
                L = S // d
                if L >= TT:
                    r = (j * TT) // L
                    i0 = (j * TT) % L
                    st0 = r + i0 * d
                    return hT[:, c, st0:st0 + (TT - 1) * d + 1:d]
                nr = TT // L
                r0 = j * nr
                base = hT[:, c, r0:r0 + 1]
                return bass.AP(base.tensor, base.offset, [list(base.ap[0]), [1, nr], [d, L]])

            wslot = [0]

            items = []

            def w_dma(it, wsrc, col0, ncols):
                slot = wslot[0] % 2
                wslot[0] += 1
                it["slot"] = slot
                self.dma(wf[slot][:, :, 0:ncols], wsrc.rearrange("(kc p) f -> p kc f", p=128)[:, :, col0:col0 + ncols],
                         [], [("wf", slot)])

            def w_cast(it, ncols):
                slot = it["slot"]
                self.copy("act", wb[slot][:, :, 0:ncols], wf[slot][:, :, 0:ncols], [("wf", slot)], [("wb", slot)])

            def rope_pair(wsrc, colA, colB, dests):
                it = {}
                it["dma"] = lambda: w_dma(it, wsrc, colA, 128)
                it["cast"] = lambda: w_cast(it, 128)
                it["compute"] = lambda hook: rope_compute(it, dests, hook)
                items.append(it)

            def rope_compute(it, dests, hook):
                slot = it["slot"]
                sgs = []
                for _ in dests:
                    sgs.append(self.cnt % len(qstg))
                    self.cnt += 1

                def front(j):
                    rb = ropeT[:, :, j * TT:(j + 1) * TT]
                    rk = ("ropeT", j)
                    b1 = self.P.bank()
                    for kc in range(8):
                        self.mm(self.psum[:, b1, :], wb[slot][:, kc, 0:128], hT[:, kc, j * TT:(j + 1) * TT],
                                kc == 0, kc == 7, [("wb", slot), ("hT", kc, j)], [("bank", b1)])
                    tb = j % NB_
                    self.copy("act", qtmp[tb][:, :], self.psum[:, b1, :], [("bank", b1)], [("qtmp", tb)])
                    self.tt("dve", t1b[tb][:, :], rb[:, 0, :], self.psum[:, b1, :], ALU.mult,
                            [rk, ("bank", b1), ("qtmp", tb)], [("t1", tb)])
                    return rb, rk

                def back(j, rb, rk):
                    tb = j % NB_
                    b2 = self.P.bank()
                    self.mm(self.psum[:, b2, :], self.permm[:, :], qtmp[tb][:, :], True, True,
                            [("permm",), ("qtmp", tb)], [("bank", b2)])
                    self.tt("dve", t2b[tb][:, :], rb[:, 1, :], self.psum[:, b2, :], ALU.mult,
                            [rk, ("bank", b2)], [("t2", tb)])
                    for (d, dst_dram, pad), sg in zip(dests, sgs):
                        stg = qstg[sg]
                        self.addn += 1
                        aeng = "pool" if self.addn % 3 else "dve"
                        if d == 1:
                            self.tt(aeng, stg[:, j * TT:(j + 1) * TT], t1b[tb][:, :], t2b[tb][:, :], ALU.add,
                                    [("t1", tb), ("t2", tb)], [("qstg", sg, j)])
                        else:
                            n_ = TT // d
                            dv = stg[:, 0:S].rearrange("p (r l) -> p r l", r=d)[:, :, j * n_:(j + 1) * n_]
                            s1 = t1b[tb][:, :].rearrange("p (i r) -> p r i", r=d)
                            s2 = t2b[tb][:, :].rearrange("p (i r) -> p r i", r=d)
                            self.tt(aeng, dv, s1, s2, ALU.add, [("t1", tb), ("t2", tb)], [("qstg", sg, j)])

                prev = None
                for j in range(NT):
                    cur = front(j)
                    if prev is not None:
                        back(j - 1, *prev)
                    prev = cur
                    if j == NT // 2:
                        hook()
                back(NT - 1, *prev)
                for (d, dst_dram, pad), sg in zip(dests, sgs):
                    stg = qstg[sg]
                    if pad:
                        self.memset("pool", stg[:, S:S + 64], 0.0, [("qstg", sg, "pad")])
                        self.dma(dst_dram, stg[:, 0:S + 64],
                                 [("qstg", sg, j) for j in range(NT)] + [("qstg", sg, "pad")], [])
                    else:
                        self.dma(dst_dram, stg[:, 0:S], [("qstg", sg, j) for j in range(NT)], [])

            def v_tiles(wsrc, col0, ncols, g, d, vcol0=0):
                it = {}
                it["dma"] = lambda: w_dma(it, wsrc, col0, ncols)
                it["cast"] = lambda: w_cast(it, ncols)
                it["compute"] = lambda hook: v_compute(it, ncols, g, d, vcol0, hook)
                items.append(it)

            def v_compute(it, ncols, g, d, vcol0, hook):
                slot = it["slot"]
                L = S // d
                tiles = []
                if even:
                    for t in range(32):
                        tiles.append((t * 128, 1, 128))
                else:
                    nb = L // 128
                    for r in range(d):
                        tiles.append((r, d, 128))
                        for k in range(1, nb):
                            tiles.append(((128 * k - 64) * d + r, d, 128))
                        tiles.append(((L - 64) * d + r, d, 64))
                for ti, (st0, stride, cnt) in enumerate(tiles):
                    if cnt < 128:
                        self.memset("pool", vstg[cnt:128, ti, 0:ncols], 0.0, [("vstgz", ti)])
                    bk = self.P.bank()
                    for kc in range(8):
                        lhsT = hT[:, kc, st0:st0 + (cnt - 1) * stride + 1:stride]
                        self.mm(self.psum[0:cnt, bk, 0:ncols], lhsT, wb[slot][:, kc, 0:ncols], kc == 0, kc == 7,
                                [("wb", slot)] + [("hT", kc, jj) for jj in range(NT)], [("bank", bk)])
                    eng = "act" if ti % 2 == 0 else "dve"
                    self.copy(eng, vstg[0:cnt, ti, 0:ncols], self.psum[0:cnt, bk, 0:ncols], [("bank", bk)],
                              [("vstg", ti)])
                    if ti == len(tiles) // 2:
                        hook()
                nt = len(tiles)
                self.dma(self.vs[s, g, :, 0:nt, vcol0:vcol0 + ncols], vstg[:, 0:nt, 0:ncols],
                         [("vstg", ti) for ti in range(nt)] + [("vstgz", ti) for ti in range(nt)],
                         [("vs", s, g, vcol0)])

            def run_items():
                for n, it in enumerate(items):
                    if n == 0:
                        it["dma"]()
                        it["cast"]()
                    nxt = items[n + 1] if n + 1 < len(items) else None
                    if nxt is not None:
                        nxt["dma"]()
                    done = [False]

                    def hook(nxt=nxt, done=done):
                        if nxt is not None and not done[0]:
                            nxt["cast"]()
                            done[0] = True
                    it["compute"](hook)
                    hook()

            if even:
                wsrc = self.wine_d[li]
                for m in range(4):
                    rope_pair(wsrc, m * 128, (4 + m) * 128, [(1, self.qs[s, m, :, :], False)])
                rope_pair(wsrc, 8 * 128, 9 * 128, [(1, self.ks[s, 0, 0, :, :], True)])
                v_tiles(wsrc, 14 * 128, 128, 0, 1)
                pst = [sb("p1ps%d" % i, [128, TT], F32) for i in range(3)]
                pz = sb("p1pz", [128, 8], F32)
                self.memset("pool", pz[:, :], 0.0, [("pz",)])
                pcn = [0]

                def p_compute(it, g, hook):
                    slot = it["slot"]
                    self.dma(self.pscr[s, g, :, 0:8], pz[:, :], [("pz",)], [])
                    self.dma(self.pscr[s, g, :, S + 8:S + 16], pz[:, :], [("pz",)], [])
                    for j in range(NT):
                        bk = self.P.bank()
                        for kc in range(8):
                            self.mm(self.psum[:, bk, :], wb[slot][:, kc, 0:128], hT[:, kc, j * TT:(j + 1) * TT],
                                    kc == 0, kc == 7, [("wb", slot), ("hT", kc, j)], [("bank", bk)])
                        pi = pcn[0] % 3
                        pcn[0] += 1
                        self.copy("act" if j % 2 == 0 else "dve", pst[pi][:, :], self.psum[:, bk, :], [("bank", bk)],
                                  [("pstg", pi)])
                        self.dma(self.pscr[s, g, :, 8 + j * TT:8 + (j + 1) * TT], pst[pi][:, :], [("pstg", pi)], [])
                        if j == NT // 2:
                            hook()

                def p_item(g):
                    it = {}
                    it["dma"] = lambda: w_dma(it, wsrc, (10 + g) * 128, 128)
                    it["cast"] = lambda: w_cast(it, 128)
                    it["compute"] = lambda hook: p_compute(it, g, hook)
                    items.append(it)
                for g in range(4):
                    p_item(g)
            else:
                wsrc = self.wino_d[li]
                for c in range(2):
                    rope_pair(wsrc, 3072 + c * 128, 3072 + (2 + c) * 128,
                              [(d, self.ks[s, g, c, :, :], True) for g, d in enumerate(DILS)])
                for g, d in enumerate(DILS):
                    for m in range(4):
                        rope_pair(wsrc, g * 1024 + m * 128, g * 1024 + (4 + m) * 128,
                                  [(d, self.qs[s, g * 4 + m, :, :], False)])
                    v_tiles(wsrc, 3072 + 512, 128, g, d, 0)
                    v_tiles(wsrc, 3072 + 512 + 128, 128, g, d, 128)
            run_items()
            self.flush()

    def attn_even(self, l, s):
        nc = self.nc
        li = l // 2
        import contextlib
        with contextlib.ExitStack() as st:
            sb = lambda name, shape, d: st.enter_context(nc.sbuf_tensor(self.uniq(name), shape, d))
            qsb = sb("aq", [128, 4, S], BF16)
            ksb = sb("ak", [128, S + 64], BF16)
            vsb = sb("av", [128, 32, 128], BF16)
            mixs = sb("amix", [128, 4, S], BF16)
            pb = [sb("apb%d" % i, [128, 3, 512], BF16) for i in range(3)]
            zt = [sb("azt%d" % i, [128, 512], F32) for i in range(2)]
            rz = [sb("arz%d" % i, [128, 512], F32) for i in range(2)]
            usb = [sb("ausb%d" % i, [128, 512], F32) for i in range(2)]
            if (l, s) in self.cast_at:
                self.bg_begin(self.cast_at[(l, s)], st)
            for m in range(4):
                self.dma(qsb[:, m, :], self.qs[s, m, :, :], [], [("aq", m)])
            self.dma(ksb[:, :], self.ks[s, 0, 0, :, :], [], [("ak",)])
            self.dma(vsb[:, :, :], self.vs[s, 0, :, 0:32, 0:128], [], [("av",)])
            its = [(i, kv) for i in range(32) for kv in range(2)]

            def tiles_of(i):
                tl = []
                if i > 0:
                    tl.append((i - 1, 0))
                tl.append((i, None))
                if i < 31:
                    tl.append((i + 1, 1))
                return tl

            def stage_a(n):
                i, kv = its[n]
                rows = slice(kv * 64, kv * 64 + 64)
                pbuf = pb[n % 3]
                for t, (kt, mi) in enumerate(tiles_of(i)):
                    bk = self.P.bank()
                    if mi is not None:
                        self.mm(self.psum[:, bk, :], self.ident[:, :], self.negE[:, mi, :], True, False,
                                [("ident",)] + [("negE", mi, h) for h in range(4)], [("bank", bk)])
                    for h in range(4):
                        self.mm(self.psum[:, bk, h * 128:(h + 1) * 128], ksb[rows, kt * 128:(kt + 1) * 128],
                                qsb[rows, h, i * 128:(i + 1) * 128], mi is None, (mi is None) or h == 3,
                                [("ak",), ("aq", h)], [("bank", bk)])
                    self.act(pbuf[:, t, :], self.psum[:, bk, :], AF.Exp, [("bank", bk)], [("pb", n % 3, t)],
                             scale=HD ** -0.5)

            def stage_b(n):
                i, kv = its[n]
                rows = slice(kv * 64, kv * 64 + 64)
                pbuf = pb[n % 3]
                tl = tiles_of(i)
                nt = len(tl)
                bu = self.P.bank()
                bz = self.P.bank()
                for t, (kt, mi) in enumerate(tl):
                    self.mm(self.psum[:, bu, :], vsb[:, kt, :], pbuf[:, t, :], t == 0, t == nt - 1,
                            [("av",), ("pb", n % 3, t)], [("bank", bu)])
                for t, (kt, mi) in enumerate(tl):
                    self.mm(self.psum[:, bz, :], self.ones[:, :], pbuf[:, t, :], t == 0, t == nt - 1,
                            [("ones",), ("pb", n % 3, t)], [("bank", bz)])
                zk = i % 2
                self.tt("dve", zt[zk][rows, :], self.esinkT[rows, li, kv, :], self.psum[rows, bz, :], ALU.add,
                        [("bank", bz)] + [("esT", li, kv, h) for h in range(4)], [("zt", zk, kv)])
                self.copy("dve", usb[zk][rows, :], self.psum[rows, bu, :], [("bank", bu)], [("usb", zk, kv)])
                if pend:
                    pend.pop()()
                if kv == 1:
                    def fin(i=i, zk=zk):
                        self.recip(rz[zk][:, :], zt[zk][:, :], self.onesf[:, :], [("zt", zk, 0), ("zt", zk, 1)],
                                   [("rz", zk)])
                        outv = mixs[:, :, i * 128:(i + 1) * 128]
                        self.tt("pool", outv, rz[zk][:, :].rearrange("p (h q) -> p h q", h=4),
                                usb[zk][:, :].rearrange("p (h q) -> p h q", h=4), ALU.mult,
                                [("rz", zk), ("usb", zk, 0), ("usb", zk, 1)], [("amix", i)])
                    pend.append(fin)

            pend = []
            stage_a(0)
            for n in range(len(its)):
                if n + 1 < len(its):
                    stage_a(n + 1)
                stage_b(n)
                self.bg_step(2)
            while pend:
                pend.pop()()
            self.bg_finish()
            for m in range(4):
                self.dma(self.mix[s, m, :, :], mixs[:, m, :], [("amix", i) for i in range(32)], [("mix", s, m)])
            self.flush()

    def attn_odd(self, l, s):
        nc = self.nc
        import contextlib
        with contextlib.ExitStack() as st:
            sb = lambda name, shape, d: st.enter_context(nc.sbuf_tensor(self.uniq(name), shape, d))
            acc = sb("oacc", [128, 2, 2, S], F32)
            qsb = [sb("oq%d" % i, [128, 2, S], BF16) for i in range(2)]
            ksb = [sb("ok%d" % i, [128, S + 64], BF16) for i in range(2)]
            vsb = [sb("ov%d" % i, [128, VT_MAX, 128], BF16) for i in range(2)]
            pb = [sb("opb%d" % i, [128, 512], BF16) for i in range(4)]
            mixs = sb("omix", [128, 2, S], BF16)
            orz = [sb("orz%d" % i, [128, TT], F32) for i in range(2)]
            if (l, s) in self.cast_at:
                self.bg_begin(self.cast_at[(l, s)], st)
            ld = 0
            for c in range(2):
                its = []
                loads = {}
                for g, d in reversed(list(enumerate(DILS))):
                    L = S // d
                    nb = L // 128
                    ntile = d * (nb + 1)
                    b = ld % 2
                    ld += 1
                    loads[len(its)] = (g, b, ntile)
                    for r in range(d):
                        for i in range(nb):
                            if i == 0:
                                left = (r * (nb + 1), r * L)
                                case = 1
                            else:
                                left = (r * (nb + 1) + i, r * L + 128 * i - 64)
                                case = 0
                            right = (r * (nb + 1) + i + 1, r * L + 128 * i + 64)
                            if i == nb - 1:
                                case = 2
                            for kv2 in range(2):
                                its.append((g, d, b, left, right, case, r * L + 128 * i, 128 * i * d + r, kv2))

                def do_loads(n, c=c):
                    g, b, ntile = loads[n]
                    for gm in range(2):
                        m = g * 4 + c * 2 + gm
                        self.dma(qsb[b][:, gm, :], self.qs[s, m, :, :], [], [("oq", b, gm)])
                    self.dma(ksb[b][:, :], self.ks[s, g, c, :, :], [], [("ok", b)])
                    self.dma(vsb[b][:, 0:ntile, :], self.vs[s, g, :, 0:ntile, c * 128:(c + 1) * 128], [], [("ov", b)])

                lstarts = sorted(loads.keys())
                pre_at = {}
                for a_, b_ in zip(lstarts[:-1], lstarts[1:]):
                    pre_at[a_ + 6] = b_
                loaded = set()

                def stage_a(n):
                    if n in loads and n not in loaded:
                        do_loads(n)
                        loaded.add(n)
                    if n in pre_at and pre_at[n] not in loaded:
                        do_loads(pre_at[n])
                        loaded.add(pre_at[n])
                    g, d, b, left, right, case, q0, nat0, kv2 = its[n]
                    rows = slice(kv2 * 64, kv2 * 64 + 64)
                    bs = self.P.bank()
                    ODD_PEMASK = (n % 2 == 0)
                    if ODD_PEMASK:
                        self.mm(self.psum[:, bs, :], self.ident[:, :], self.neg3[:, case, :], True, False,
                                [("ident",)] + [("neg3", case, t, h) for t in range(2) for h in range(2)],
                                [("bank", bs)])
                    for t, (vi, col0) in enumerate((left, right)):
                        for gm in range(2):
                            cc = (t * 2 + gm) * 128
                            self.mm(self.psum[:, bs, cc:cc + 128], ksb[b][rows, col0:col0 + 128],
                                    qsb[b][rows, gm, q0:q0 + 128], not ODD_PEMASK,
                                    (t == 1 and gm == 1) or not ODD_PEMASK,
                                    [("ok", b), ("oq", b, gm)], [("bank", bs)])
                    self.act(pb[n % 4][:, :], self.psum[:, bs, :], AF.Exp, [("bank", bs)], [("opb", n % 4)],
                             scale=HD ** -0.5)
                    if not ODD_PEMASK:
                        self.tt("pool", pb[n % 4][:, :], pb[n % 4][:, :], self.mask3[:, case, :], ALU.mult,
                                [("opb", n % 4)] + [("mask3", case, t, h) for t in range(2) for h in range(2)],
                                [("opb", n % 4)])

                def stage_b(n):
                    g, d, b, left, right, case, q0, nat0, kv2 = its[n]
                    rows = slice(kv2 * 64, kv2 * 64 + 64)
                    pbuf = pb[n % 4]
                    pk = ("opb", n % 4)
                    bu = self.P.bank()
                    for t, (vi, col0) in enumerate((left, right)):
                        self.mm(self.psum[:, bu, 0:256], vsb[b][:, vi, :], pbuf[:, t * 256:(t + 1) * 256],
                                t == 0, t == 1, [("ov", b), pk], [("bank", bu)])
                    for t in range(2):
                        self.mm(self.psum[:, bu, 256:512], self.ones[:, :], pbuf[:, t * 256:(t + 1) * 256],
                                t == 0, t == 1, [("ones",), pk], [("bank", bu)])
                    dst = acc[rows, :, :, nat0:nat0 + 127 * d + 1:d]
                    srcv = self.psum[rows, bu, :].rearrange("p (z h q) -> p z h q", z=2, h=2)
                    key = ("oacc", kv2)
                    if g == len(DILS) - 1:
                        self.copy("dve", dst, srcv, [("bank", bu)], [key], noself=True)
                    else:
                        self.tt("dve", dst, dst, srcv, ALU.add, [("bank", bu), key], [key], noself=True)

                def norm_slice(j, c=c):
                    sl = slice(j * TT, (j + 1) * TT)
                    for gm in range(2):
                        self.recip(orz[gm][:, :], acc[:, 1, gm, sl], self.onesf[:, :],
                                   [("oacc", 0), ("oacc", 1), ("onesf",)], [("orz", gm)])
                        self.tt("dve", mixs[:, gm, sl], acc[:, 0, gm, sl], orz[gm][:, :], ALU.mult,
                                [("orz", gm), ("oacc", 0), ("oacc", 1)], [("omix", gm, j)])
                        self.dma(self.mix[s, c * 2 + gm, :, sl], mixs[:, gm, sl], [("omix", gm, j)], [])

                def after_b(n):
                    g, d, b, left, right, case, q0, nat0, kv2 = its[n]
                    if g == 0 and kv2 == 1:
                        i = q0 // 128
                        if i % 4 == 3:
                            norm_slice(i // 4)

                if ODD_PIPE:
                    stage_a(0)
                    stage_a(1)
                    for n in range(len(its)):
                        if n + 2 < len(its):
                            stage_a(n + 2)
                        stage_b(n)
                        after_b(n)
                        if n % 3 == 0:
                            self.bg_step(1)
                else:
                    for n in range(len(its)):
                        stage_a(n)
                        stage_b(n)
                        after_b(n)
                if c == 1:
                    self.bg_finish()
            self.flush()

    def phase34(self, l, s):
        nc = self.nc
        even = (l % 2 == 0)
        li = l // 2
        last = (l == self.n_layers - 1)
        nk = 8 if even else 4
        src = self.xin if l == 0 else self.xs
        import contextlib
        with contextlib.ExitStack() as st:
            sb = lambda name, shape, d: st.enter_context(nc.sbuf_tensor(self.uniq(name), shape, d))
            xb = [sb("x3_%d" % i, [128, 8, TT], F32) for i in range(2)]
            self.dma(xb[0][:, :, :], src[s].rearrange("(c p) t -> p c t", p=128)[:, :, 0:TT], [],
                     [("x3", 0, c) for c in range(8)])
            wof = [sb("wof%d" % i, [128, 1024], F32) for i in range(2)]
            wout = sb("wout", [128, nk, 1024], BF16)
            wsrc = (self.woute_d if even else self.wouto_d)[li]
            for kc in range(min(2, nk)):
                self.dma(wof[kc % 2][:, :], wsrc[kc * 128:(kc + 1) * 128, :], [], [("wof", kc % 2)])
            for kc in range(nk):
                self.copy("act", wout[:, kc, :], wof[kc % 2][:, :], [("wof", kc % 2)], [("wout", kc)])
                if kc + 2 < nk:
                    self.dma(wof[kc % 2][:, :], wsrc[(kc + 2) * 128:(kc + 3) * 128, :], [], [("wof", kc % 2)])
            if even:
                wpf = sb("wpf", [128, 4, 128], F32)
                wpool = sb("wpool", [128, 4, 128], BF16)
                self.dma(wpf[:, :, :], self.wpool_d[li].rearrange("g p c -> p g c"), [], [("wpf",)])
                self.copy("act", wpool[:, :, :], wpf[:, :, :], [("wpf",)], [("wpool",)])
                ptl = sb("ptl", [128, 4, TT + 16], F32)
                pa = sb("ppa", [128, TT + 16], F32)
                pbb = sb("ppb", [128, TT + 16], F32)
                dD = [sb("pdD%d" % i, [128, TT], BF16) for i in range(4)]
            mt = sb("mixt", [128, 8, TT], BF16)
            hT = sb("h3", [128, 8, TT], BF16)
            sq = sb("sq3", [128, 8, TT], BF16)
            rt = sb("rt3", [128, TT], F32)
            rstd = sb("rstd3", [128, TT], F32)
            uT = sb("uT", [128, 32, TT], BF16)
            wu = [sb("wu%d" % i, [128, 8, 512], BF16) for i in range(3)]
            wd = [sb("wd%d" % i, [128, 32, 128], BF16) for i in range(2)]
            rl = [sb("rl%d" % i, [128, TT], F32) for i in range(3)]
            if last:
                yo = sb("yo", [128, 8, TT], F32)
            cn = {"wu": 0, "wd": 0, "rl": 0}
            mk = ("mixt",)

            def xkeyf(j):
                return ("x3", j % 2)

            def load_x(j):
                xt = xb[j % 2]
                tok = slice(j * TT, (j + 1) * TT)
                self.dma(xt[:, :, :], src[s].rearrange("(c p) t -> p c t", p=128)[:, :, tok], [("xres", s, j)],
                         [xkeyf(j) + (c,) for c in range(8)])

            def stage_a1(j):
                tok = slice(j * TT, (j + 1) * TT)
                self.dma(mt[:, 0:4, :], self.mix[s, 0:4].rearrange("c p t -> p c t")[:, :, tok], [],
                         [mk + (c,) for c in range(4)])
                if not even:
                    return
                pt = ptl
                self.dma(pt[:, :, :], self.pscr[s].rearrange("g p t -> p g t")[:, :, j * TT:j * TT + TT + 16], [],
                         [("ptl",)])
                for g, w in enumerate((2, 4, 8, 16)):
                    p_g = pt[:, g, :]
                    W = TT + 16
                    self.tt("pool", pa[:, 1:W], p_g[:, 0:W - 1], p_g[:, 1:W], ALU.add, [("ptl",)], [("ppa",)])
                    cur, curk, oth, othk = pa, ("ppa",), pbb, ("ppb",)
                    sh = 1
                    lo, hi = 1, W
                    while sh * 2 < w:
                        lo2, hi2 = lo + sh, hi - sh
                        self.tt("pool", oth[:, lo2:hi2], cur[:, lo2 - sh:hi2 - sh], cur[:, lo2 + sh:hi2 + sh],
                                ALU.add, [curk], [othk])
                        cur, curk, oth, othk = oth, othk, cur, curk
                        lo, hi = lo2, hi2
                        sh *= 2
                    dt_ = dD[g]
                    dk = ("pdD", g)
                    self.stt("dve", dt_[:, :], cur[:, 8:8 + TT], 1.0 / w, p_g[:, 8:8 + TT], ALU.mult, ALU.subtract,
                             [curk, ("ptl",)], [dk])
                    fix = []
                    if j == 0:
                        for t in range(w // 2):
                            fix.append((t, t + w // 2))
                    if j == NT - 1:
                        for t in range(S - w // 2 + 1, S):
                            fix.append((t - j * TT, S - t + w // 2))
                    for (tc, cntv) in fix:
                        self.stt("dve", dt_[:, tc:tc + 1], cur[:, 8 + tc:9 + tc], 1.0 / cntv, p_g[:, 8 + tc:9 + tc],
                                 ALU.mult, ALU.subtract, [curk, ("ptl",), dk], [dk])

            def stage_a2(j):
                xt = xb[j % 2]
                xk = xkeyf(j)
                if even:
                    for g in range(4):
                        bk = self.P.bank()
                        self.mm(self.psum[:, bk, :], wpool[:, g, :], dD[g][:, :], True, True, [("wpool",), ("pdD", g)],
                                [("bank", bk)])
                        self.act(mt[:, 4 + g, :], self.psum[:, bk, :], AF.Copy, [("bank", bk), ("prm",)],
                                 [mk + (4 + g,)], scale=self.prm[:, 72 + li * 4 + g:72 + li * 4 + g + 1])
                for dc in range(8):
                    bk = self.P.bank()
                    for kc in range(nk):
                        self.mm(self.psum[:, bk, :], wout[:, kc, dc * 128:(dc + 1) * 128], mt[:, kc, :], kc == 0,
                                kc == nk - 1, [("wout", kc), mk + (kc,)], [("bank", bk)])
                    self.tt("dve", xt[:, dc, :], xt[:, dc, :], self.psum[:, bk, :], ALU.add,
                            [xk + (dc,), ("bank", bk)], [xk + (dc,)])
                self.norm_tile(xt, 32 + l * 8, lambda c: hT[:, c, :], ("n3",), sq, rt, rstd,
                               lambda c, xk=xk: xk + (c,), lambda c: ("n3", "h", c))

            winfo = {}

            def issue_wu(j, gq):
                slot = cn["wu"] % len(wu)
                cn["wu"] += 1
                wk = ("wu", slot)
                self.dma(wu[slot][:, :, :], self.wus[l, gq].rearrange("p (kc f) -> p kc f", kc=8), [], [wk])
                winfo[("u", j, gq)] = (wu[slot], wk)

            def issue_wd(j, dc):
                slot = cn["wd"] % len(wd)
                cn["wd"] += 1
                wk = ("wd", slot)
                self.dma(wd[slot][:, :, :], self.wds[l, dc].rearrange("p (f d) -> p f d", f=32), [], [wk])
                winfo[("d", j, dc)] = (wd[slot], wk)

            def prefetch_up(j):
                for gq in range(len(wu)):
                    issue_wu(j, gq)

            def prefetch_down(j):
                for dc in range(len(wd)):
                    issue_wd(j, dc)

            def stage_up(j):
                for gq in range(8):
                    wb_, wk = winfo.pop(("u", j, gq))
                    for f4 in range(4):
                        f = gq * 4 + f4
                        bk = self.P.bank()
                        for kc in range(8):
                            self.mm(self.psum[:, bk, :], wb_[:, kc, f4 * 128:(f4 + 1) * 128], hT[:, kc, :], kc == 0,
                                    kc == 7, [wk, ("n3", "h", kc)], [("bank", bk)])
                        r_ = rl[cn["rl"] % 3]
                        rk = ("rl", cn["rl"] % 3)
                        cn["rl"] += 1
                        self.act(r_[:, :], self.psum[:, bk, :], AF.Relu, [("bank", bk)], [rk])
                        self.tt("dve" if f % 2 == 0 else "pool", uT[:, f, :], r_[:, :], r_[:, :], ALU.mult, [rk],
                                [("uT", f)])
                    if gq + len(wu) < 8:
                        issue_wu(j, gq + len(wu))

            def stage_down(j):
                xt = xb[j % 2]
                xk = xkeyf(j)
                tok = slice(j * TT, (j + 1) * TT)
                for dc in range(8):
                    wb_, wk = winfo.pop(("d", j, dc))
                    bk = self.P.bank()
                    for f in range(32):
                        self.mm(self.psum[:, bk, :], wb_[:, f, :], uT[:, f, :], f == 0, f == 31, [wk, ("uT", f)],
                                [("bank", bk)])
                    self.tt("dve", xt[:, dc, :], xt[:, dc, :], self.psum[:, bk, :], ALU.add,
                            [xk + (dc,), ("bank", bk)], [xk + (dc,)])
                    if dc + len(wd) < 8:
                        issue_wd(j, dc + len(wd))
                if not last:
                    self.dma(self.xs[s].rearrange("(c p) t -> p c t", p=128)[:, :, tok], xt[:, :, :],
                             [xk + (c,) for c in range(8)], [("xres", s, j)])
                    self.norm_tile(xt, (l + 1) * 8, lambda c: uT[:, c, :], ("n3",), sq, rt, rstd,
                                   lambda c, xk=xk: xk + (c,), lambda c: ("uT", c))
                    self.dma(self.hs[s][:, :, tok], uT[:, 0:8, :], [("uT", c) for c in range(8)], [])
                else:
                    self.norm_tile(xt, 64, lambda c: yo[:, c, :], ("n3",), sq, rt, rstd,
                                   lambda c, xk=xk: xk + (c,), lambda c: ("yo", c))
                    self.dma(self.yout[s].rearrange("(c p) t -> p c t", p=128)[:, :, tok], yo[:, :, :],
                             [("yo", c) for c in range(8)], [("yout", s, j)])

            stage_a1(0)
            stage_a2(0)
            prefetch_up(0)
            if NT > 1:
                stage_a1(1)
                load_x(1)
            stage_up(0)
            for j in range(1, NT):
                prefetch_down(j - 1)
                stage_a2(j)
                prefetch_up(j)
                if j + 1 < NT:
                    stage_a1(j + 1)
                stage_down(j - 1)
                if j + 1 < NT:
                    load_x(j + 1)
                stage_up(j)
            prefetch_down(NT - 1)
            stage_down(NT - 1)
            self.flush()


def _rope_tables():
    theta = np.float32(500000.0)
    inv = (theta ** (-np.arange(0, 16, 2, dtype=np.float32) / np.float32(16))).astype(np.float32)
    pos = np.arange(S, dtype=np.float32)
    ang = (pos[:, None] * inv[None, :]).astype(np.float32)
    cos = np.cos(ang).astype(np.float32)
    sin = np.sin(ang).astype(np.float32)
    C = np.ones((64, S), np.float32)
    Sg = np.zeros((64, S), np.float32)
    C[0:8] = cos.T
    C[8:16] = cos.T
    Sg[0:8] = -sin.T
    Sg[8:16] = sin.T
    C = np.concatenate([C, C], 0)
    Sg = np.concatenate([Sg, Sg], 0)
    out = np.zeros((3, 128, 2, S), np.float32)
    for g, d in enumerate(DILS):
        L = S // d
        tp = np.arange(S)
        nat = (tp % L) * d + tp // L
        out[g, :, 0, :] = C[:, nat]
        out[g, :, 1, :] = Sg[:, nat]
    return out


def _masks():
    b = np.arange(128)[:, None]
    a = np.arange(128)[None, :]
    m = np.zeros((128, 6, 128), np.float32)
    m[:, 4, :] = (a == b)
    for mcol in range(128):
        i = mcol % 64
        if i < 8:
            m[mcol + 8, 5, mcol] = 1.0
        elif i < 16:
            m[mcol - 8, 5, mcol] = 1.0
    m[:, 0, :] = (a <= b)
    m[:, 1, :] = (b <= a)
    m[:, 2, :] = (b < 64) & (a - b <= 64)
    m[:, 3, :] = (b < 64) & (b <= a)
    return m


def _partner(cols):
    p = np.array(cols).copy()
    p[0:8] = cols[8:16]
    p[8:16] = cols[0:8]
    return p


def _prep_weights(inp):
    li_e = []
    for i in range(2):
        w = inp["w_in_even"][i]
        cols = []
        pcols = []
        for m in range(4):
            for h in (m, 4 + m):
                hc = np.arange(h * 64, h * 64 + 64)
                cols.append(hc)
                pcols.append(_partner(hc))
        kc = []
        kpc = []
        for kh in range(2):
            hc = np.arange(512 + kh * 64, 512 + kh * 64 + 64)
            kc.append(hc)
            kpc.append(_partner(hc))
        order = np.concatenate(cols + pcols + kc + kpc + [np.arange(768, 1280)] + [np.arange(640, 768)])
        li_e.append(w[:, order])
    w_in_e = np.ascontiguousarray(np.stack(li_e))
    rows = []
    for m in range(4):
        rows += [np.arange(m * 64, m * 64 + 64), np.arange((4 + m) * 64, (4 + m) * 64 + 64)]
    rows.append(np.arange(512, 1024))
    rows = np.concatenate(rows)
    w_out_e = np.ascontiguousarray(inp["w_out_even"][:, rows, :])
    li_o = []
    for i in range(2):
        w = inp["w_in_odd"][i]
        allc = []
        for g in range(3):
            cols = []
            pcols = []
            for c in range(2):
                for gm in range(2):
                    for kv in (2 * c, 2 * c + 1):
                        base = ((g * 4 + kv) * 2 + gm) * 64
                        hc = np.arange(base, base + 64)
                        cols.append(hc)
                        pcols.append(_partner(hc))
            allc += cols + pcols
        kc = []
        kpc = []
        for kh in range(4):
            hc = np.arange(1536 + kh * 64, 1536 + kh * 64 + 64)
            kc.append(hc)
            kpc.append(_partner(hc))
        order = np.concatenate(allc + kc + kpc + [np.arange(1792, 2048)])
        li_o.append(w[:, order])
    w_in_o = np.ascontiguousarray(np.stack(li_o))
    rows = []
    for c in range(2):
        for gm in range(2):
            for kv in (2 * c, 2 * c + 1):
                base = (kv * 2 + gm) * 64
                rows.append(np.arange(base, base + 64))
    rows = np.concatenate(rows)
    w_out_o = np.ascontiguousarray(inp["w_out_odd"][:, rows, :])
    return w_in_e, w_out_e, w_in_o, w_out_o


def _prm(inp):
    prm = np.zeros((128, 96), np.float32)
    for l in range(4):
        prm[:, l * 8:(l + 1) * 8] = inp["norm_mix"][l].reshape(8, 128).T
        prm[:, 32 + l * 8:32 + (l + 1) * 8] = inp["norm_mlp"][l].reshape(8, 128).T
    prm[:, 64:72] = inp["norm_final"].reshape(8, 128).T
    for i in range(2):
        prm[:, 72 + i * 4:72 + (i + 1) * 4] = inp["pool_scale"][i].reshape(4, 128).T
        prm[:, 80 + i * 8:80 + (i + 1) * 8] = inp["sink_logits"][i][None, :]
    return prm


_CACHE = {}


def make_in_maps(inp, ncores, nseq):
    inp = {k: np.asarray(v) for k, v in inp.items()}
    w_in_e, w_out_e, w_in_o, w_out_o = _prep_weights(inp)
    shared = {
        "prm": _prm(inp), "rope": _rope_tables(), "masks": _masks(),
        "w_in_e": w_in_e, "w_out_e": w_out_e, "w_pool": np.ascontiguousarray(inp["w_pool"]),
        "w_in_o": w_in_o, "w_out_o": w_out_o,
        "w_up": np.ascontiguousarray(inp["w_up"]), "w_down": np.ascontiguousarray(inp["w_down"]),
    }
    x = inp["x"]
    maps = []
    for c in range(ncores):
        xT = np.ascontiguousarray(np.transpose(x[c * nseq:(c + 1) * nseq], (0, 2, 1)))
        m = dict(shared)
        m["xT"] = xT
        maps.append(m)
    return maps


def kernel(**inputs):
    ncores = 8
    b = Builder(DEPTH, NSEQ)
    nc = b.build()
    maps = make_in_maps(inputs, ncores, NSEQ)
    res = run_bass_kernel_spmd(nc, maps, core_ids=list(range(ncores)))
    outs = [np.transpose(r["yT"], (0, 2, 1)) for r in res.results]
    return np.ascontiguousarray(np.concatenate(outs, axis=0)).astype(np.float32)
```

```python
import numpy as np
import concourse.bass as bass
import concourse.mybir as mybir
from concourse.bass_utils import run_bass_kernel_spmd

F32 = mybir.dt.float32
BF16 = mybir.dt.bfloat16
ALU = mybir.AluOpType
AF = mybir.ActivationFunctionType

D = 1024
S = 4096
NSEQ = 2
DEPTH = 4
DFF = 4096
HD = 64
EPS = 1e-6
TT = 512
NT = S // TT
DILS = (1, 4, 16)
VT_MAX = 48
N_DMA_SEMS = 24
RECIP_POOL = False
ODD_PIPE = True
USE_RSQRT = False
USE_LNEXP = True
ODD_PEMASK = True
PERM_ENG = 'pool'
EVEN_COLS = 14 * 128 + 128
ODD_COLS = 3 * 1024 + 512 + 256


class Op:
    __slots__ = ("eng", "fn", "deps", "flag", "sem", "val", "snap", "is_dma", "idx", "noself")

    def __init__(self, eng, fn, is_dma):
        self.eng = eng
        self.fn = fn
        self.deps = ()
        self.flag = False
        self.sem = None
        self.val = 0
        self.snap = None
        self.is_dma = is_dma
        self.idx = 0
        self.noself = False


class Prog:
    ENG = ("pe", "act", "dve", "pool", "sp")

    def __init__(self, nc, eng_sems, dma_sems, same_engine_sync=True):
        self.nc = nc
        self.ops = []
        self.state = {}
        self.ses = same_engine_sync
        self.eng_sems = eng_sems
        self.dma_sems = dma_sems
        self.emitted = 0
        self.known = {e: {} for e in self.ENG}
        self.counts = {e: 0 for e in self.ENG}
        self.dma_hist = [None] * len(dma_sems)
        self.dma_vals = [0] * len(dma_sems)
        self.dma_n = 0
        self.last_op = {}
        self.dma_since = []
        self.pending = {}
        self.nbank = 0

    def bank(self):
        b = self.nbank % 8
        self.nbank += 1
        return b

    def add(self, eng, fn, reads=(), writes=(), is_dma=False, noself=False):
        op = Op(eng, fn, is_dma)
        op.noself = noself
        op.idx = len(self.ops)
        deps = {}
        st = self.state
        for k in reads:
            s = st.get(k)
            if s is not None:
                for w in s[0]:
                    deps[w.idx] = w
        for k in writes:
            s = st.get(k)
            if s is not None:
                for w in s[0]:
                    deps[w.idx] = w
                for r in s[1]:
                    deps[r.idx] = r
        pend = self.pending.pop(eng, None)
        if pend:
            for d in pend:
                deps[d.idx] = d
        op.deps = list(deps.values())
        for k in writes:
            st[k] = [[op], []]
        for k in reads:
            s = st.get(k)
            if s is None:
                s = st[k] = [[], []]
            rl = s[1]
            if not is_dma:
                rl[:] = [r for r in rl if r.is_dma or r.eng != eng]
            rl.append(op)
        self.ops.append(op)
        if is_dma:
            self.dma_since.append(op)
        else:
            self.last_op[eng] = op
        return op

    def barrier(self):
        deps = list(self.last_op.values()) + self.dma_since
        for d_ in deps:
            d_.flag = True
        self.dma_since = []
        for e in self.ENG:
            self.pending[e] = list(deps) + self.pending.get(e, [])
        self.state = {}

    def emit(self):
        ses = self.ses
        new = self.ops[self.emitted:]
        for op in new:
            for d in op.deps:
                if d.is_dma or op.is_dma or d.eng != op.eng:
                    d.flag = True
                elif ses and op.eng != "pe" and not op.noself:
                    d.flag = True
        streams = {e: [] for e in self.ENG}
        nd = len(self.dma_sems)
        for op in new:
            e = op.eng
            kn = self.known[e]
            st = streams[e]
            for d in op.deps:
                if d.sem is None:
                    if d.idx < self.emitted:
                        raise RuntimeError("dependency on unflagged emitted op")
                    continue
                if (not d.is_dma) and (not op.is_dma) and d.eng == e and (e == "pe" or not ses or op.noself):
                    continue
                sid = id(d.sem)
                if kn.get(sid, 0) >= d.val:
                    continue
                st.append(("wait", d.sem, d.val))
                for k2, v2 in d.snap.items():
                    if kn.get(k2, 0) < v2:
                        kn[k2] = v2
                kn[sid] = d.val
            if op.is_dma:
                slot = self.dma_n % nd
                self.dma_n += 1
                prev = self.dma_hist[slot]
                sem = self.dma_sems[slot]
                if prev is not None and kn.get(id(sem), 0) < prev.val:
                    st.append(("wait", sem, prev.val))
                    kn[id(sem)] = prev.val
                self.dma_vals[slot] += 16
                op.sem = sem
                op.val = self.dma_vals[slot]
                self.dma_hist[slot] = op
                op.snap = dict(kn)
                st.append(("dma", op.fn, sem))
            else:
                if op.flag:
                    sem = self.eng_sems[e]
                    self.counts[e] += 1
                    op.sem = sem
                    op.val = self.counts[e]
                    op.snap = dict(kn)
                    op.snap[id(sem)] = op.val
                    st.append(("op", op.fn, sem))
                else:
                    st.append(("op", op.fn, None))
        self.emitted = len(self.ops)
        return streams


def replay(engine, items):
    for it in items:
        if it[0] == "wait":
            engine.wait_ge(it[1], it[2])
        elif it[0] == "dma":
            it[1](engine).then_inc(it[2], 16)
        else:
            ins = it[1](engine)
            if it[2] is not None:
                ins.then_inc(it[2], 1)


def bc(ap_tensor, offset, dims):
    return bass.AP(ap_tensor, offset, dims)


class Builder:
    def __init__(self, n_layers=DEPTH, nseq=NSEQ, debug_dump=None):
        self.n_layers = n_layers
        self.nseq = nseq
        self.debug_dump = debug_dump
        self.nc = bass.Bass("TRN2", target_bir_lowering=False)

    def uniq(self, name):
        self._u = getattr(self, '_u', 0) + 1
        return '%s_%d' % (name, self._u)

    def mm(self, out, lhsT, rhs, start, stop, reads, writes):
        self.P.add("pe", lambda e, o=out, l=lhsT, r=rhs, a=start, b=stop: e.matmul(o, l, r, start=a, stop=b),
                   reads, writes)

    def act(self, out, in_, func, reads, writes, bias=None, scale=None):
        kw = {}
        if bias is not None:
            kw["bias"] = bias
        if scale is not None:
            kw["scale"] = scale
        self.P.add("act", lambda e, o=out, i=in_, f=func, k=kw: e.activation(o, i, f, **k), reads, writes)

    def tt(self, eng, out, in0, in1, op, reads, writes, noself=False):
        self.P.add(eng, lambda e, o=out, a=in0, b=in1, p=op: e.tensor_tensor(o, a, b, p), reads, writes,
                   noself=noself)

    def stt(self, eng, out, in0, scalar, in1, op0, op1, reads, writes):
        self.P.add(eng, lambda e, o=out, a=in0, s=scalar, b=in1, p0=op0, p1=op1:
                   e.scalar_tensor_tensor(o, a, s, b, p0, p1), reads, writes)

    def copy(self, eng, out, in_, reads, writes, noself=False):
        if eng == "act":
            self.P.add("act", lambda e, o=out, i=in_: e.copy(o, i), reads, writes, noself=noself)
        else:
            self.P.add(eng, lambda e, o=out, i=in_: e.tensor_copy(o, i), reads, writes, noself=noself)

    def recip(self, out, in_, ones, reads, writes):
        if USE_LNEXP:
            self.P.add("act", lambda e, o=out, i=in_: e.activation(o, i, AF.Ln), reads, writes)
            self.P.add("act", lambda e, o=out: e.activation(o, o, AF.Exp, scale=-1.0), writes, writes)
        elif RECIP_POOL:
            self.P.add("pool", lambda e, o=out, i=in_, on=ones: e.tensor_tensor(o, i, on, ALU.pow), reads, writes)
        else:
            self.P.add("dve", lambda e, o=out, i=in_: e.reciprocal(o, i), reads, writes)

    def memset(self, eng, ap, val, writes):
        self.P.add(eng, lambda e, a=ap, v=val: e.memset(a, v), (), writes)

    def dma(self, out, in_, reads, writes):
        self.P.add("sp", lambda e, o=out, i=in_: e.dma_start(out=o, in_=i), reads, writes, is_dma=True)

    def flush(self, final=False):
        self.P.barrier()
        streams = self.P.emit()
        if final:
            sp = streams["sp"]
            for slot, h in enumerate(self.P.dma_hist):
                if h is not None:
                    sp.append(("wait", h.sem, h.val))
        with self.nc.Block() as block:
            @block.tensor
            def _(e):
                replay(e, streams["pe"])

            @block.scalar
            def _(e):
                replay(e, streams["act"])

            @block.vector
            def _(e):
                replay(e, streams["dve"])

            @block.gpsimd
            def _(e):
                replay(e, streams["pool"])

            @block.sync
            def _(e):
                replay(e, streams["sp"])

    def build(self):
        nc = self.nc
        ns = self.nseq
        dt = nc.dram_tensor
        self.xin = dt("xT", [ns, D, S], F32, kind="ExternalInput").ap()
        self.prm_d = dt("prm", [128, 96], F32, kind="ExternalInput").ap()
        self.rope_d = dt("rope", [3, 128, 2, S], F32, kind="ExternalInput").ap()
        self.mask_d = dt("masks", [128, 6, 128], F32, kind="ExternalInput").ap()
        self.wine_d = dt("w_in_e", [2, D, EVEN_COLS], F32, kind="ExternalInput").ap()
        self.woute_d = dt("w_out_e", [2, D, D], F32, kind="ExternalInput").ap()
        self.wpool_d = dt("w_pool", [2, 4, 128, 128], F32, kind="ExternalInput").ap()
        self.wino_d = dt("w_in_o", [2, D, ODD_COLS], F32, kind="ExternalInput").ap()
        self.wouto_d = dt("w_out_o", [2, 512, D], F32, kind="ExternalInput").ap()
        self.wup_d = dt("w_up", [DEPTH, D, DFF], F32, kind="ExternalInput").ap()
        self.wdn_d = dt("w_down", [DEPTH, DFF, D], F32, kind="ExternalInput").ap()
        self.yout = dt("yT", [ns, D, S], F32, kind="ExternalOutput").ap()
        self.xs = dt("xs", [ns, D, S], F32, kind="Internal").ap()
        self.qs = dt("qs", [ns, 12, 128, S], BF16, kind="Internal").ap()
        self.ks = dt("ks", [ns, 3, 2, 128, S + 64], BF16, kind="Internal").ap()
        self.vs = dt("vs", [ns, 3, 128, VT_MAX, 256], BF16, kind="Internal").ap()
        self.pscr = dt("pscr", [ns, 4, 128, S + 16], F32, kind="Internal").ap()
        self.mix = dt("mix", [ns, 8, 128, S], BF16, kind="Internal").ap()
        self.hs = dt("hs", [ns, 128, 8, S], BF16, kind="Internal").ap()
        self.wus = dt("wus", [DEPTH, 8, 128, 8 * 512], BF16, kind="Internal").ap()
        self.wds = dt("wds", [DEPTH, 8, 128, 32 * 128], BF16, kind="Internal").ap()
        if self.debug_dump:
            self.dbg = dt("dbg", [ns, D, S], F32, kind="ExternalOutput").ap()

        import contextlib
        with contextlib.ExitStack() as top:
            esems = {e: top.enter_context(nc.semaphore("sem_" + e)) for e in Prog.ENG}
            dsems = [top.enter_context(nc.semaphore("dsem%d" % i)) for i in range(N_DMA_SEMS)]
            self.P = Prog(nc, esems, dsems)
            sb = lambda name, shape, d: top.enter_context(nc.sbuf_tensor(name, shape, d))
            self.psum = top.enter_context(nc.psum_tensor("psum", [128, 8, 512], F32))
            self.prm = sb("prm_sb", [128, 96], F32)
            self.ones = sb("ones", [128, 128], BF16)
            self.epsb = sb("epsb", [128, 1], F32)
            self.onesf = sb("onesf", [128, TT], F32)
            self.maskf = sb("maskf", [128, 6, 128], F32)
            self.ident = sb("ident", [128, 128], BF16)
            self.permm = sb("permm", [128, 128], BF16)
            self.negb = sb("negb", [128, 4, 128], BF16)
            self.neg3 = sb("neg3", [128, 3, 512], BF16)
            self.negE = sb("negE", [128, 2, 512], BF16)
            self.esink = sb("esink", [128, 2, 16], F32)
            self.esinkT = sb("esinkT", [128, 2, 2, 512], F32)
            with nc.allow_low_precision("bf16 matmul operands by design"):
                self.setup()
                self.flush()
                self.bg = None
                self.cast_at = {}
                for l in range(self.n_layers):
                    if l % 2 == 1 or l == 0 or ns < 2:
                        self.cast_at[(l, 0)] = l
                    else:
                        self.cast_at[(l - 1, ns - 1)] = l
                for l in range(self.n_layers):
                    for s in range(ns):
                        self.phase1(l, s)
                        self.P.barrier()
                        self.flush()
                        if l % 2 == 0:
                            self.attn_even(l, s)
                        else:
                            self.attn_odd(l, s)
                        self.P.barrier()
                        self.flush()
                        self.phase34(l, s)
                        self.P.barrier()
                        self.flush()
                self.P.barrier()
                self.memset("dve", self.epsb[:, 0:1], EPS, [("eps",)])
                self.flush(final=True)
        return nc

    def setup(self):
        P = self.P
        self.dma(self.prm[:, :], self.prm_d[:, :], [], [("prm",)])
        self.dma(self.maskf[:, :, :], self.mask_d[:, :, :], [], [("maskf",)])
        self.memset("dve", self.ones[:, :], 1.0, [("ones",)])
        self.memset("dve", self.epsb[:, :], EPS, [("eps",)])
        self.memset("dve", self.onesf[:, :], -1.0 if RECIP_POOL else 1.0, [("onesf",)])
        self.copy("dve", self.ident[:, :], self.maskf[:, 4, :], [("maskf",)], [("ident",)])
        self.copy("dve", self.permm[:, :], self.maskf[:, 5, :], [("maskf",)], [("permm",)])
        self.P.add("dve", lambda e: e.tensor_scalar(self.negb[:, :, :], self.maskf[:, 0:4, :], -1.0, 30000.0, ALU.add,
                                                    ALU.mult), [("maskf",)], [("negb",)])
        combos = ((0, 1), (2, 1), (0, 3))
        for ci, (ml, mr) in enumerate(combos):
            for t, mi in enumerate((ml, mr)):
                for h in range(2):
                    self.copy("dve", self.neg3[:, ci, (t * 2 + h) * 128:(t * 2 + h + 1) * 128],
                              self.negb[:, mi, :], [("negb",)], [("neg3", ci, t, h)])
        for mi in range(2):
            for h in range(4):
                self.copy("dve", self.negE[:, mi, h * 128:(h + 1) * 128], self.negb[:, mi, :], [("negb",)],
                          [("negE", mi, h)])
        self.act(self.esink[:, :, 0:8], self.prm[:, 80:96].rearrange("p (i h) -> p i h", i=2), AF.Exp,
                 [("prm",)], [("esink",)])
        for i in range(2):
            for kv in range(2):
                for h in range(4):
                    dst = self.esinkT[:, i, kv, h * 128:(h + 1) * 128]
                    self.memset("dve", dst, 0.0, [("esT", i, kv, h)])
                    hh = kv * 4 + h
                    self.P.add("dve", lambda e, o=dst, s=self.esink[:, i, hh:hh + 1]:
                               e.tensor_scalar(o, o, s, None, ALU.add), [("esink",), ("esT", i, kv, h)],
                               [("esT", i, kv, h)])

    def cast_steps(self, l, wf, wb):
        groups = []
        for kind in ("up", "down"):
            for g in range(8):
                for hf in range(2):
                    groups.append((kind, g, hf))

        def aps(n):
            kind, g, hf = groups[n]
            b = n % 2
            if kind == "up":
                src = self.wup_d[l].rearrange("(kc p) f -> p kc f", p=128)[:, hf * 4:(hf + 1) * 4,
                                                                         g * 512:(g + 1) * 512]
                dstv = wf[b][:, :].rearrange("p (kc f) -> p kc f", kc=4)
                dst_d = self.wus[l, g][:, hf * 2048:(hf + 1) * 2048]
            else:
                src = self.wdn_d[l].rearrange("(f p) d -> p f d", p=128)[:, hf * 16:(hf + 1) * 16,
                                                                         g * 128:(g + 1) * 128]
                dstv = wf[b][:, :].rearrange("p (f d) -> p f d", f=16)
                dst_d = self.wds[l, g][:, hf * 2048:(hf + 1) * 2048]
            return b, src, dstv, dst_d

        b, src, dstv, dst_d = aps(0)
        self.dma(dstv, src, [], [("cwf", b)])
        yield
        for n in range(len(groups)):
            b, src, dstv, dst_d = aps(n)
            if n + 1 < len(groups):
                b2, src2, dstv2, _ = aps(n + 1)
                self.dma(dstv2, src2, [], [("cwf", b2)])
            yield
            for q in range(2):
                sl = slice(q * 1024, (q + 1) * 1024)
                self.copy("pool" if (q + n) % 2 == 0 else "act", wb[b][:, sl], wf[b][:, sl], [("cwf", b)],
                          [("cwb", b, q)])
                yield
            self.dma(dst_d, wb[b][:, :], [("cwb", b, q) for q in range(2)], [])
            yield

    def bg_begin(self, l, st):
        nc = self.nc
        wf = [st.enter_context(nc.sbuf_tensor(self.uniq("cwf%d" % i), [128, 2048], F32)) for i in range(2)]
        wb = [st.enter_context(nc.sbuf_tensor(self.uniq("cwb%d" % i), [128, 2048], BF16)) for i in range(2)]
        self.bg = self.cast_steps(l, wf, wb)

    def bg_step(self, k=1):
        if self.bg is None:
            return
        for _ in range(k):
            try:
                next(self.bg)
            except StopIteration:
                self.bg = None
                return

    def bg_finish(self):
        while self.bg is not None:
            self.bg_step(1)

    def norm_tile(self, xt, gcol, hout, tag, sq, rt, rstd, xkey, hkey):
        bk = self.P.bank()
        ps = self.psum[:, bk, :]
        for c in range(8):
            self.act(sq[:, c, :], xt[:, c, :], AF.Square, [xkey(c)], [(tag, "sq", c)])
        for c in range(8):
            self.mm(ps, self.ones[:, :], sq[:, c, :], c == 0, c == 7, [(tag, "sq", c), ("ones",)], [("bank", bk)])
        if USE_LNEXP:
            self.act(rt[:, :], ps, AF.Ln, [("bank", bk), ("eps",)], [(tag, "rt")], bias=self.epsb[:, 0:1],
                     scale=1.0 / D)
            self.act(rstd[:, :], rt[:, :], AF.Exp, [(tag, "rt")], [(tag, "rstd")], scale=-0.5)
        elif USE_RSQRT:
            self.act(rstd[:, :], ps, AF.Abs_reciprocal_sqrt, [("bank", bk), ("eps",)], [(tag, "rstd")],
                     bias=self.epsb[:, 0:1], scale=1.0 / D)
        else:
            self.act(rt[:, :], ps, AF.Sqrt, [("bank", bk), ("eps",)], [(tag, "rt")], bias=self.epsb[:, 0:1],
                     scale=1.0 / D)
            self.recip(rstd[:, :], rt[:, :], self.onesf[:, :], [(tag, "rt"), ("onesf",)], [(tag, "rstd")])
        for c in range(8):
            self.stt("dve", hout(c), xt[:, c, :], self.prm[:, gcol + c:gcol + c + 1], rstd[:, :], ALU.mult, ALU.mult,
                     [xkey(c), (tag, "rstd"), ("prm",)], [hkey(c)])

    def phase1(self, l, s):
        nc = self.nc
        even = (l % 2 == 0)
        li = l // 2
        src = self.xin if l == 0 else self.xs
        import contextlib
        with contextlib.ExitStack() as st:
            sb = lambda name, shape, d: st.enter_context(nc.sbuf_tensor(self.uniq(name), shape, d))
            hT = sb("hT", [128, 8, S], BF16)
            if l == 0:
                with contextlib.ExitStack() as st2:
                    sb2 = lambda name, shape, d: st2.enter_context(nc.sbuf_tensor(self.uniq(name), shape, d))
                    xb = [sb2("p1x%d" % i, [128, 8, TT], F32) for i in range(2)]
                    sq = sb2("p1sq", [128, 8, TT], BF16)
                    rt = sb2("p1rt", [128, TT], F32)
                    rstd = sb2("p1rstd", [128, TT], F32)
                    for j in range(NT):
                        xt = xb[j % 2]
                        xtag = ("n1x", j % 2)
                        self.dma(xt[:, :, :], src[s].rearrange("(c p) t -> p c t", p=128)[:, :, j * TT:(j + 1) * TT],
                                 [("xres", s, j)], [(xtag, "x", c) for c in range(8)])
                        self.norm_tile(xt, l * 8, lambda c, j=j: hT[:, c, j * TT:(j + 1) * TT], ("n1",), sq, rt, rstd,
                                       lambda c, xtag=xtag: (xtag, "x", c), lambda c, j=j: ("hT", c, j))
                    self.flush()
            else:
                pass
            wf = [sb("p1wf%d" % i, [128, 8, 256], F32) for i in range(2)]
            wb = [sb("p1wb%d" % i, [128, 8, 256], BF16) for i in range(2)]
            ropeT = sb("p1ropeT", [128, 2, S], F32)

            def bulk_loads():
                for j in range(NT):
                    if l > 0:
                        self.dma(hT[:, :, j * TT:(j + 1) * TT], self.hs[s][:, :, j * TT:(j + 1) * TT], [],
                                 [("hT", c, j) for c in range(8)])
                    self.dma(ropeT[:, :, j * TT:(j + 1) * TT], self.rope_d[0][:, :, j * TT:(j + 1) * TT], [],
                             [("ropeT", j)])
            NB_ = 4
            t1b = [sb("p1t1_%d" % i, [128, TT], F32) for i in range(NB_)]
            t2b = [sb("p1t2_%d" % i, [128, TT], F32) for i in range(NB_)]
            qtmp = [sb("p1qt%d" % i, [128, TT], BF16) for i in range(NB_)]
            qstg = [sb("p1qs%d" % i, [128, S + 64], BF16) for i in range(2 if even else 4)]
            self.ropen = 0
            self.addn = 0
            vstg = sb("p1vs", [128, 32 if even else VT_MAX, 128], BF16)
            self.cnt = getattr(self, "cnt", 0)
            hkeys = [("hT", c, j) for c in range(8) for j in range(NT)]

            def load_w(wsrc, col0, ncols, slot):
                self.dma(wf[slot][:, :, 0:ncols], wsrc.rearrange("(kc p) f -> p kc f", p=128)[:, :, col0:col0 + ncols],
                         [], [("wf", slot)])
                self.copy("act", wb[slot][:, :, 0:ncols], wf[slot][:, :, 0:ncols], [("wf", slot)], [("wb", slot)])

            def tok_ap(c, j, d):
                if d == 1:
                    return hT[:, c, j * TT:(j + 1) * TT]
                L = S // d
                if L >= TT:
                    r = (j * TT) // L
                    i0 = (j * TT) % L
                    st0 = r + i0 * d
                    return hT[:, c, st0:st0 + (TT - 1) * d + 1:d]
                nr = TT // L
                r0 = j * nr
                base = hT[:, c, r0:r0 + 1]
                return bass.AP(base.tensor, base.offset, [list(base.ap[0]), [1, nr], [d, L]])

            wslot = [0]

            items = []

            def w_dma(it, wsrc, col0, ncols):
                slot = wslot[0] % 2
                wslot[0] += 1
                it["slot"] = slot
                self.dma(wf[slot][:, :, 0:ncols], wsrc.rearrange("(kc p) f -> p kc f", p=128)[:, :, col0:col0 + ncols],
                         [], [("wf", slot)])

            def w_cast(it, ncols):
                slot = it["slot"]
                self.copy("act", wb[slot][:, :, 0:ncols], wf[slot][:, :, 0:ncols], [("wf", slot)], [("wb", slot)])

            def rope_pair(wsrc, colA, colB, dests):
                it = {}
                it["dma"] = lambda: w_dma(it, wsrc, colA, 128)
                it["cast"] = lambda: w_cast(it, 128)
                it["compute"] = lambda hook: rope_compute(it, dests, hook)
                items.append(it)

            def rope_compute(it, dests, hook):
                slot = it["slot"]
                sgs = []
                for _ in dests:
                    sgs.append(self.cnt % len(qstg))
                    self.cnt += 1

                def front(j):
                    rb = ropeT[:, :, j * TT:(j + 1) * TT]
                    rk = ("ropeT", j)
                    b1 = self.P.bank()
                    for kc in range(8):
                        self.mm(self.psum[:, b1, :], wb[slot][:, kc, 0:128], hT[:, kc, j * TT:(j + 1) * TT],
                                kc == 0, kc == 7, [("wb", slot), ("hT", kc, j)], [("bank", b1)])
                    tb = j % NB_
                    self.copy("act", qtmp[tb][:, :], self.psum[:, b1, :], [("bank", b1)], [("qtmp", tb)])
                    self.tt("dve", t1b[tb][:, :], rb[:, 0, :], self.psum[:, b1, :], ALU.mult,
                            [rk, ("bank", b1), ("qtmp", tb)], [("t1", tb)])
                    return rb, rk

                def back(j, rb, rk):
                    tb = j % NB_
                    b2 = self.P.bank()
                    self.mm(self.psum[:, b2, :], self.permm[:, :], qtmp[tb][:, :], True, True,
                            [("permm",), ("qtmp", tb)], [("bank", b2)])
                    self.tt("dve", t2b[tb][:, :], rb[:, 1, :], self.psum[:, b2, :], ALU.mult,
                            [rk, ("bank", b2)], [("t2", tb)])
                    for (d, dst_dram, pad), sg in zip(dests, sgs):
                        stg = qstg[sg]
                        self.addn += 1
                        aeng = "pool" if self.addn % 3 else "dve"
                        if d == 1:
                            self.tt(aeng, stg[:, j * TT:(j + 1) * TT], t1b[tb][:, :], t2b[tb][:, :], ALU.add,
                                    [("t1", tb), ("t2", tb)], [("qstg", sg, j)])
                        else:
                            n_ = TT // d
                            dv = stg[:, 0:S].rearrange("p (r l) -> p r l", r=d)[:, :, j * n_:(j + 1) * n_]
                            s1 = t1b[tb][:, :].rearrange("p (i r) -> p r i", r=d)
                            s2 = t2b[tb][:, :].rearrange("p (i r) -> p r i", r=d)
                            self.tt(aeng, dv, s1, s2, ALU.add, [("t1", tb), ("t2", tb)], [("qstg", sg, j)])

                prev = None
                for j in range(NT):
                    cur = front(j)
                    if prev is not None:
                        back(j - 1, *prev)
                    prev = cur
                    if j == NT // 2:
                        hook()
                back(NT - 1, *prev)
                for (d, dst_dram, pad), sg in zip(dests, sgs):
                    stg = qstg[sg]
                    if pad:
                        self.memset("pool", stg[:, S:S + 64], 0.0, [("qstg", sg, "pad")])
                        self.dma(dst_dram, stg[:, 0:S + 64],
                                 [("qstg", sg, j) for j in range(NT)] + [("qstg", sg, "pad")], [])
                    else:
                        self.dma(dst_dram, stg[:, 0:S], [("qstg", sg, j) for j in range(NT)], [])

            def v_tiles(wsrc, col0, ncols, g, d, vcol0=0):
                it = {}
                it["dma"] = lambda: w_dma(it, wsrc, col0, ncols)
                it["cast"] = lambda: w_cast(it, ncols)
                it["compute"] = lambda hook: v_compute(it, ncols, g, d, vcol0, hook)
                items.append(it)

            def v_compute(it, ncols, g, d, vcol0, hook):
                slot = it["slot"]
                L = S // d
                tiles = []
                if even:
                    for t in range(32):
                        tiles.append((t * 128, 1, 128))
                else:
                    nb = L // 128
                    for r in range(d):
                        tiles.append((r, d, 128))
                        for k in range(1, nb):
                            tiles.append(((128 * k - 64) * d + r, d, 128))
                        tiles.append(((L - 64) * d + r, d, 64))
                for ti, (st0, stride, cnt) in enumerate(tiles):
                    if cnt < 128:
                        self.memset("pool", vstg[cnt:128, ti, 0:ncols], 0.0, [("vstgz", ti)])
                    bk = self.P.bank()
                    for kc in range(8):
                        lhsT = hT[:, kc, st0:st0 + (cnt - 1) * stride + 1:stride]
                        self.mm(self.psum[0:cnt, bk, 0:ncols], lhsT, wb[slot][:, kc, 0:ncols], kc == 0, kc == 7,
                                [("wb", slot)] + [("hT", kc, jj) for jj in range(NT)], [("bank", bk)])
                    eng = "act" if ti % 2 == 0 else "dve"
                    self.copy(eng, vstg[0:cnt, ti, 0:ncols], self.psum[0:cnt, bk, 0:ncols], [("bank", bk)],
                              [("vstg", ti)])
                    if ti == len(tiles) // 2:
                        hook()
                nt = len(tiles)
                self.dma(self.vs[s, g, :, 0:nt, vcol0:vcol0 + ncols], vstg[:, 0:nt, 0:ncols],
                         [("vstg", ti) for ti in range(nt)] + [("vstgz", ti) for ti in range(nt)],
                         [("vs", s, g, vcol0)])

            def run_items():
                for n, it in enumerate(items):
                    if n == 0:
                        it["dma"]()
                        bulk_loads()
                        it["cast"]()
                    nxt = items[n + 1] if n + 1 < len(items) else None
                    if nxt is not None:
                        nxt["dma"]()
                    done = [False]

                    def hook(nxt=nxt, done=done):
                        if nxt is not None and not done[0]:
                            nxt["cast"]()
                            done[0] = True
                    it["compute"](hook)
                    hook()

            if even:
                wsrc = self.wine_d[li]
                for m in range(4):
                    rope_pair(wsrc, m * 128, (4 + m) * 128, [(1, self.qs[s, m, :, :], False)])
                rope_pair(wsrc, 8 * 128, 9 * 128, [(1, self.ks[s, 0, 0, :, :], True)])
                v_tiles(wsrc, 14 * 128, 128, 0, 1)
                pst = [sb("p1ps%d" % i, [128, TT], F32) for i in range(3)]
                pz = sb("p1pz", [128, 8], F32)
                self.memset("pool", pz[:, :], 0.0, [("pz",)])
                pcn = [0]

                def p_compute(it, g, hook):
                    slot = it["slot"]
                    self.dma(self.pscr[s, g, :, 0:8], pz[:, :], [("pz",)], [])
                    self.dma(self.pscr[s, g, :, S + 8:S + 16], pz[:, :], [("pz",)], [])
                    for j in range(NT):
                        bk = self.P.bank()
                        for kc in range(8):
                            self.mm(self.psum[:, bk, :], wb[slot][:, kc, 0:128], hT[:, kc, j * TT:(j + 1) * TT],
                                    kc == 0, kc == 7, [("wb", slot), ("hT", kc, j)], [("bank", bk)])
                        pi = pcn[0] % 3
                        pcn[0] += 1
                        self.copy("act" if j % 2 == 0 else "dve", pst[pi][:, :], self.psum[:, bk, :], [("bank", bk)],
                                  [("pstg", pi)])
                        self.dma(self.pscr[s, g, :, 8 + j * TT:8 + (j + 1) * TT], pst[pi][:, :], [("pstg", pi)], [])
                        if j == NT // 2:
                            hook()

                def p_item(g):
                    it = {}
                    it["dma"] = lambda: w_dma(it, wsrc, (10 + g) * 128, 128)
                    it["cast"] = lambda: w_cast(it, 128)
                    it["compute"] = lambda hook: p_compute(it, g, hook)
                    items.append(it)
                for g in range(4):
                    p_item(g)
            else:
                wsrc = self.wino_d[li]
                for c in range(2):
                    rope_pair(wsrc, 3072 + c * 128, 3072 + (2 + c) * 128,
                              [(d, self.ks[s, g, c, :, :], True) for g, d in enumerate(DILS)])
                for g, d in enumerate(DILS):
                    for m in range(4):
                        rope_pair(wsrc, g * 1024 + m * 128, g * 1024 + (4 + m) * 128,
                                  [(d, self.qs[s, g * 4 + m, :, :], False)])
                    v_tiles(wsrc, 3072 + 512, 128, g, d, 0)
                    v_tiles(wsrc, 3072 + 512 + 128, 128, g, d, 128)
            run_items()
            self.flush()

    def attn_even(self, l, s):
        nc = self.nc
        li = l // 2
        import contextlib
        with contextlib.ExitStack() as st:
            sb = lambda name, shape, d: st.enter_context(nc.sbuf_tensor(self.uniq(name), shape, d))
            qsb = sb("aq", [128, 4, S], BF16)
            ksb = sb("ak", [128, S + 64], BF16)
            vsb = sb("av", [128, 32, 128], BF16)
            mixs = sb("amix", [128, 4, S], BF16)
            pb = [sb("apb%d" % i, [128, 3, 512], BF16) for i in range(3)]
            zt = [sb("azt%d" % i, [128, 512], F32) for i in range(2)]
            rz = [sb("arz%d" % i, [128, 512], F32) for i in range(2)]
            usb = [sb("ausb%d" % i, [128, 512], F32) for i in range(2)]
            if (l, s) in self.cast_at:
                self.bg_begin(self.cast_at[(l, s)], st)
            for m in range(4):
                self.dma(qsb[:, m, :], self.qs[s, m, :, :], [], [("aq", m)])
            self.dma(ksb[:, :], self.ks[s, 0, 0, :, :], [], [("ak",)])
            self.dma(vsb[:, :, :], self.vs[s, 0, :, 0:32, 0:128], [], [("av",)])
            its = [(i, kv) for i in range(32) for kv in range(2)]

            def tiles_of(i):
                tl = []
                if i > 0:
                    tl.append((i - 1, 0))
                tl.append((i, None))
                if i < 31:
                    tl.append((i + 1, 1))
                return tl

            def stage_a(n):
                i, kv = its[n]
                rows = slice(kv * 64, kv * 64 + 64)
                pbuf = pb[n % 3]
                for t, (kt, mi) in enumerate(tiles_of(i)):
                    bk = self.P.bank()
                    if mi is not None:
                        self.mm(self.psum[:, bk, :], self.ident[:, :], self.negE[:, mi, :], True, False,
                                [("ident",)] + [("negE", mi, h) for h in range(4)], [("bank", bk)])
                    for h in range(4):
                        self.mm(self.psum[:, bk, h * 128:(h + 1) * 128], ksb[rows, kt * 128:(kt + 1) * 128],
                                qsb[rows, h, i * 128:(i + 1) * 128], mi is None, (mi is None) or h == 3,
                                [("ak",), ("aq", h)], [("bank", bk)])
                    self.act(pbuf[:, t, :], self.psum[:, bk, :], AF.Exp, [("bank", bk)], [("pb", n % 3, t)],
                             scale=HD ** -0.5)

            def stage_b(n):
                i, kv = its[n]
                rows = slice(kv * 64, kv * 64 + 64)
                pbuf = pb[n % 3]
                tl = tiles_of(i)
                nt = len(tl)
                bu = self.P.bank()
                bz = self.P.bank()
                for t, (kt, mi) in enumerate(tl):
                    self.mm(self.psum[:, bu, :], vsb[:, kt, :], pbuf[:, t, :], t == 0, t == nt - 1,
                            [("av",), ("pb", n % 3, t)], [("bank", bu)])
                for t, (kt, mi) in enumerate(tl):
                    self.mm(self.psum[:, bz, :], self.ones[:, :], pbuf[:, t, :], t == 0, t == nt - 1,
                            [("ones",), ("pb", n % 3, t)], [("bank", bz)])
                zk = i % 2
                self.tt("dve", zt[zk][rows, :], self.esinkT[rows, li, kv, :], self.psum[rows, bz, :], ALU.add,
                        [("bank", bz)] + [("esT", li, kv, h) for h in range(4)], [("zt", zk, kv)])
                self.copy("dve", usb[zk][rows, :], self.psum[rows, bu, :], [("bank", bu)], [("usb", zk, kv)])
                if pend:
                    pend.pop()()
                if kv == 1:
                    def fin(i=i, zk=zk):
                        self.recip(rz[zk][:, :], zt[zk][:, :], self.onesf[:, :], [("zt", zk, 0), ("zt", zk, 1)],
                                   [("rz", zk)])
                        outv = mixs[:, :, i * 128:(i + 1) * 128]
                        self.tt("pool", outv, rz[zk][:, :].rearrange("p (h q) -> p h q", h=4),
                                usb[zk][:, :].rearrange("p (h q) -> p h q", h=4), ALU.mult,
                                [("rz", zk), ("usb", zk, 0), ("usb", zk, 1)], [("amix", i)])
                    pend.append(fin)

            pend = []
            stage_a(0)
            for n in range(len(its)):
                if n + 1 < len(its):
                    stage_a(n + 1)
                stage_b(n)
                self.bg_step(2)
            while pend:
                pend.pop()()
            self.bg_finish()
            for m in range(4):
                self.dma(self.mix[s, m, :, :], mixs[:, m, :], [("amix", i) for i in range(32)], [("mix", s, m)])
            self.flush()

    def attn_odd(self, l, s):
        nc = self.nc
        import contextlib
        with contextlib.ExitStack() as st:
            sb = lambda name, shape, d: st.enter_context(nc.sbuf_tensor(self.uniq(name), shape, d))
            acc = sb("oacc", [128, 2, 2, S], F32)
            qsb = [sb("oq%d" % i, [128, 2, S], BF16) for i in range(2)]
            ksb = [sb("ok%d" % i, [128, S + 64], BF16) for i in range(2)]
            vsb = [sb("ov%d" % i, [128, VT_MAX, 128], BF16) for i in range(2)]
            pb = [sb("opb%d" % i, [128, 512], BF16) for i in range(4)]
            mixs = sb("omix", [128, 2, S], BF16)
            orz = [sb("orz%d" % i, [128, TT], F32) for i in range(2)]
            if (l, s) in self.cast_at:
                self.bg_begin(self.cast_at[(l, s)], st)
            ld = 0
            for c in range(2):
                its = []
                loads = {}
                for g, d in reversed(list(enumerate(DILS))):
                    L = S // d
                    nb = L // 128
                    ntile = d * (nb + 1)
                    b = ld % 2
                    ld += 1
                    loads[len(its)] = (g, b, ntile)
                    for r in range(d):
                        for i in range(nb):
                            if i == 0:
                                left = (r * (nb + 1), r * L)
                                case = 1
                            else:
                                left = (r * (nb + 1) + i, r * L + 128 * i - 64)
                                case = 0
                            right = (r * (nb + 1) + i + 1, r * L + 128 * i + 64)
                            if i == nb - 1:
                                case = 2
                            for kv2 in range(2):
                                its.append((g, d, b, left, right, case, r * L + 128 * i, 128 * i * d + r, kv2))

                def do_loads(n, c=c):
                    g, b, ntile = loads[n]
                    for gm in range(2):
                        m = g * 4 + c * 2 + gm
                        self.dma(qsb[b][:, gm, :], self.qs[s, m, :, :], [], [("oq", b, gm)])
                    self.dma(ksb[b][:, :], self.ks[s, g, c, :, :], [], [("ok", b)])
                    self.dma(vsb[b][:, 0:ntile, :], self.vs[s, g, :, 0:ntile, c * 128:(c + 1) * 128], [], [("ov", b)])

                lstarts = sorted(loads.keys())
                pre_at = {}
                for a_, b_ in zip(lstarts[:-1], lstarts[1:]):
                    pre_at[a_ + 6] = b_
                loaded = set()

                def stage_a(n):
                    if n in loads and n not in loaded:
                        do_loads(n)
                        loaded.add(n)
                    if n in pre_at and pre_at[n] not in loaded:
                        do_loads(pre_at[n])
                        loaded.add(pre_at[n])
                    g, d, b, left, right, case, q0, nat0, kv2 = its[n]
                    rows = slice(kv2 * 64, kv2 * 64 + 64)
                    bs = self.P.bank()
                    if ODD_PEMASK:
                        self.mm(self.psum[:, bs, :], self.ident[:, :], self.neg3[:, case, :], True, False,
                                [("ident",)] + [("neg3", case, t, h) for t in range(2) for h in range(2)],
                                [("bank", bs)])
                    for t, (vi, col0) in enumerate((left, right)):
                        for gm in range(2):
                            cc = (t * 2 + gm) * 128
                            self.mm(self.psum[:, bs, cc:cc + 128], ksb[b][rows, col0:col0 + 128],
                                    qsb[b][rows, gm, q0:q0 + 128], not ODD_PEMASK,
                                    (t == 1 and gm == 1) or not ODD_PEMASK,
                                    [("ok", b), ("oq", b, gm)], [("bank", bs)])
                    self.act(pb[n % 4][:, :], self.psum[:, bs, :], AF.Exp, [("bank", bs)], [("opb", n % 4)],
                             scale=HD ** -0.5)
                    if not ODD_PEMASK:
                        self.tt("pool", pb[n % 4][:, :], pb[n % 4][:, :], self.mask3[:, case, :], ALU.mult,
                                [("opb", n % 4)] + [("mask3", case, t, h) for t in range(2) for h in range(2)],
                                [("opb", n % 4)])

                def stage_b(n):
                    g, d, b, left, right, case, q0, nat0, kv2 = its[n]
                    rows = slice(kv2 * 64, kv2 * 64 + 64)
                    pbuf = pb[n % 4]
                    pk = ("opb", n % 4)
                    bu = self.P.bank()
                    for t, (vi, col0) in enumerate((left, right)):
                        self.mm(self.psum[:, bu, 0:256], vsb[b][:, vi, :], pbuf[:, t * 256:(t + 1) * 256],
                                t == 0, t == 1, [("ov", b), pk], [("bank", bu)])
                    for t in range(2):
                        self.mm(self.psum[:, bu, 256:512], self.ones[:, :], pbuf[:, t * 256:(t + 1) * 256],
                                t == 0, t == 1, [("ones",), pk], [("bank", bu)])
                    dst = acc[rows, :, :, nat0:nat0 + 127 * d + 1:d]
                    srcv = self.psum[rows, bu, :].rearrange("p (z h q) -> p z h q", z=2, h=2)
                    key = ("oacc", kv2)
                    if g == len(DILS) - 1:
                        self.copy("dve", dst, srcv, [("bank", bu)], [key], noself=True)
                    else:
                        self.tt("dve", dst, dst, srcv, ALU.add, [("bank", bu), key], [key], noself=True)

                def norm_slice(j, c=c):
                    sl = slice(j * TT, (j + 1) * TT)
                    for gm in range(2):
                        self.recip(orz[gm][:, :], acc[:, 1, gm, sl], self.onesf[:, :],
                                   [("oacc", 0), ("oacc", 1), ("onesf",)], [("orz", gm)])
                        self.tt("dve", mixs[:, gm, sl], acc[:, 0, gm, sl], orz[gm][:, :], ALU.mult,
                                [("orz", gm), ("oacc", 0), ("oacc", 1)], [("omix", gm, j)])
                        self.dma(self.mix[s, c * 2 + gm, :, sl], mixs[:, gm, sl], [("omix", gm, j)], [])

                def after_b(n):
                    g, d, b, left, right, case, q0, nat0, kv2 = its[n]
                    if g == 0 and kv2 == 1:
                        i = q0 // 128
                        if i % 4 == 3:
                            norm_slice(i // 4)

                if ODD_PIPE:
                    stage_a(0)
                    stage_a(1)
                    for n in range(len(its)):
                        if n + 2 < len(its):
                            stage_a(n + 2)
                        stage_b(n)
                        after_b(n)
                        if n % 3 == 0:
                            self.bg_step(1)
                else:
                    for n in range(len(its)):
                        stage_a(n)
                        stage_b(n)
                        after_b(n)
                if c == 1:
                    self.bg_finish()
            self.flush()

    def phase34(self, l, s):
        nc = self.nc
        even = (l % 2 == 0)
        li = l // 2
        last = (l == self.n_layers - 1)
        nk = 8 if even else 4
        src = self.xin if l == 0 else self.xs
        import contextlib
        with contextlib.ExitStack() as st:
            sb = lambda name, shape, d: st.enter_context(nc.sbuf_tensor(self.uniq(name), shape, d))
            xb = [sb("x3_%d" % i, [128, 8, TT], F32) for i in range(2)]
            self.dma(xb[0][:, :, :], src[s].rearrange("(c p) t -> p c t", p=128)[:, :, 0:TT], [],
                     [("x3", 0, c) for c in range(8)])
            wof = [sb("wof%d" % i, [128, 1024], F32) for i in range(2)]
            wout = sb("wout", [128, nk, 1024], BF16)
            wsrc = (self.woute_d if even else self.wouto_d)[li]
            for kc in range(min(2, nk)):
                self.dma(wof[kc % 2][:, :], wsrc[kc * 128:(kc + 1) * 128, :], [], [("wof", kc % 2)])
            for kc in range(nk):
                self.copy("act", wout[:, kc, :], wof[kc % 2][:, :], [("wof", kc % 2)], [("wout", kc)])
                if kc + 2 < nk:
                    self.dma(wof[kc % 2][:, :], wsrc[(kc + 2) * 128:(kc + 3) * 128, :], [], [("wof", kc % 2)])
            if even:
                wpf = sb("wpf", [128, 4, 128], F32)
                wpool = sb("wpool", [128, 4, 128], BF16)
                self.dma(wpf[:, :, :], self.wpool_d[li].rearrange("g p c -> p g c"), [], [("wpf",)])
                self.copy("act", wpool[:, :, :], wpf[:, :, :], [("wpf",)], [("wpool",)])
                ptl = sb("ptl", [128, 4, TT + 16], F32)
                pa = sb("ppa", [128, TT + 16], F32)
                pbb = sb("ppb", [128, TT + 16], F32)
                dD = [sb("pdD%d" % i, [128, TT], BF16) for i in range(4)]
            mt = sb("mixt", [128, 8, TT], BF16)
            hT = sb("h3", [128, 8, TT], BF16)
            sq = sb("sq3", [128, 8, TT], BF16)
            rt = sb("rt3", [128, TT], F32)
            rstd = sb("rstd3", [128, TT], F32)
            uT = sb("uT", [128, 32, TT], BF16)
            wu = [sb("wu%d" % i, [128, 8, 512], BF16) for i in range(3)]
            wd = [sb("wd%d" % i, [128, 32, 128], BF16) for i in range(2)]
            rl = [sb("rl%d" % i, [128, TT], F32) for i in range(3)]
            if last:
                yo = sb("yo", [128, 8, TT], F32)
            cn = {"wu": 0, "wd": 0, "rl": 0}
            mk = ("mixt",)

            def xkeyf(j):
                return ("x3", j % 2)

            def load_x(j):
                xt = xb[j % 2]
                tok = slice(j * TT, (j + 1) * TT)
                self.dma(xt[:, :, :], src[s].rearrange("(c p) t -> p c t", p=128)[:, :, tok], [("xres", s, j)],
                         [xkeyf(j) + (c,) for c in range(8)])

            def stage_a1(j):
                tok = slice(j * TT, (j + 1) * TT)
                self.dma(mt[:, 0:4, :], self.mix[s, 0:4].rearrange("c p t -> p c t")[:, :, tok], [],
                         [mk + (c,) for c in range(4)])
                if not even:
                    return
                pt = ptl
                self.dma(pt[:, :, :], self.pscr[s].rearrange("g p t -> p g t")[:, :, j * TT:j * TT + TT + 16], [],
                         [("ptl",)])
                for g, w in enumerate((2, 4, 8, 16)):
                    p_g = pt[:, g, :]
                    W = TT + 16
                    self.tt("pool", pa[:, 1:W], p_g[:, 0:W - 1], p_g[:, 1:W], ALU.add, [("ptl",)], [("ppa",)])
                    cur, curk, oth, othk = pa, ("ppa",), pbb, ("ppb",)
                    sh = 1
                    lo, hi = 1, W
                    while sh * 2 < w:
                        lo2, hi2 = lo + sh, hi - sh
                        self.tt("pool", oth[:, lo2:hi2], cur[:, lo2 - sh:hi2 - sh], cur[:, lo2 + sh:hi2 + sh],
                                ALU.add, [curk], [othk])
                        cur, curk, oth, othk = oth, othk, cur, curk
                        lo, hi = lo2, hi2
                        sh *= 2
                    dt_ = dD[g]
                    dk = ("pdD", g)
                    self.stt("dve", dt_[:, :], cur[:, 8:8 + TT], 1.0 / w, p_g[:, 8:8 + TT], ALU.mult, ALU.subtract,
                             [curk, ("ptl",)], [dk])
                    fix = []
                    if j == 0:
                        for t in range(w // 2):
                            fix.append((t, t + w // 2))
                    if j == NT - 1:
                        for t in range(S - w // 2 + 1, S):
                            fix.append((t - j * TT, S - t + w // 2))
                    for (tc, cntv) in fix:
                        self.stt("dve", dt_[:, tc:tc + 1], cur[:, 8 + tc:9 + tc], 1.0 / cntv, p_g[:, 8 + tc:9 + tc],
                                 ALU.mult, ALU.subtract, [curk, ("ptl",), dk], [dk])

            def stage_a2(j):
                xt = xb[j % 2]
                xk = xkeyf(j)
                if even:
                    for g in range(4):
                        bk = self.P.bank()
                        self.mm(self.psum[:, bk, :], wpool[:, g, :], dD[g][:, :], True, True, [("wpool",), ("pdD", g)],
                                [("bank", bk)])
                        self.act(mt[:, 4 + g, :], self.psum[:, bk, :], AF.Copy, [("bank", bk), ("prm",)],
                                 [mk + (4 + g,)], scale=self.prm[:, 72 + li * 4 + g:72 + li * 4 + g + 1])
                for dc in range(8):
                    bk = self.P.bank()
                    for kc in range(nk):
                        self.mm(self.psum[:, bk, :], wout[:, kc, dc * 128:(dc + 1) * 128], mt[:, kc, :], kc == 0,
                                kc == nk - 1, [("wout", kc), mk + (kc,)], [("bank", bk)])
                    self.tt("dve", xt[:, dc, :], xt[:, dc, :], self.psum[:, bk, :], ALU.add,
                            [xk + (dc,), ("bank", bk)], [xk + (dc,)])
                self.norm_tile(xt, 32 + l * 8, lambda c: hT[:, c, :], ("n3",), sq, rt, rstd,
                               lambda c, xk=xk: xk + (c,), lambda c: ("n3", "h", c))

            winfo = {}

            def issue_wu(j, gq):
                slot = cn["wu"] % len(wu)
                cn["wu"] += 1
                wk = ("wu", slot)
                self.dma(wu[slot][:, :, :], self.wus[l, gq].rearrange("p (kc f) -> p kc f", kc=8), [], [wk])
                winfo[("u", j, gq)] = (wu[slot], wk)

            def issue_wd(j, dc):
                slot = cn["wd"] % len(wd)
                cn["wd"] += 1
                wk = ("wd", slot)
                self.dma(wd[slot][:, :, :], self.wds[l, dc].rearrange("p (f d) -> p f d", f=32), [], [wk])
                winfo[("d", j, dc)] = (wd[slot], wk)

            def prefetch_up(j):
                for gq in range(len(wu)):
                    issue_wu(j, gq)

            def prefetch_down(j):
                for dc in range(len(wd)):
                    issue_wd(j, dc)

            def stage_up(j):
                for gq in range(8):
                    wb_, wk = winfo.pop(("u", j, gq))
                    for f4 in range(4):
                        f = gq * 4 + f4
                        bk = self.P.bank()
                        for kc in range(8):
                            self.mm(self.psum[:, bk, :], wb_[:, kc, f4 * 128:(f4 + 1) * 128], hT[:, kc, :], kc == 0,
                                    kc == 7, [wk, ("n3", "h", kc)], [("bank", bk)])
                        r_ = rl[cn["rl"] % 3]
                        rk = ("rl", cn["rl"] % 3)
                        cn["rl"] += 1
                        self.act(r_[:, :], self.psum[:, bk, :], AF.Relu, [("bank", bk)], [rk])
                        self.tt("dve" if f % 2 == 0 else "pool", uT[:, f, :], r_[:, :], r_[:, :], ALU.mult, [rk],
                                [("uT", f)])
                    if gq + len(wu) < 8:
                        issue_wu(j, gq + len(wu))

            def stage_down(j):
                xt = xb[j % 2]
                xk = xkeyf(j)
                tok = slice(j * TT, (j + 1) * TT)
                for dc in range(8):
                    wb_, wk = winfo.pop(("d", j, dc))
                    bk = self.P.bank()
                    for f in range(32):
                        self.mm(self.psum[:, bk, :], wb_[:, f, :], uT[:, f, :], f == 0, f == 31, [wk, ("uT", f)],
                                [("bank", bk)])
                    self.tt("dve", xt[:, dc, :], xt[:, dc, :], self.psum[:, bk, :], ALU.add,
                            [xk + (dc,), ("bank", bk)], [xk + (dc,)])
                    if dc + len(wd) < 8:
                        issue_wd(j, dc + len(wd))
                if not last:
                    self.dma(self.xs[s].rearrange("(c p) t -> p c t", p=128)[:, :, tok], xt[:, :, :],
                             [xk + (c,) for c in range(8)], [("xres", s, j)])
                    self.norm_tile(xt, (l + 1) * 8, lambda c: uT[:, c, :], ("n3",), sq, rt, rstd,
                                   lambda c, xk=xk: xk + (c,), lambda c: ("uT", c))
                    self.dma(self.hs[s][:, :, tok], uT[:, 0:8, :], [("uT", c) for c in range(8)], [])
                else:
                    self.norm_tile(xt, 64, lambda c: yo[:, c, :], ("n3",), sq, rt, rstd,
                                   lambda c, xk=xk: xk + (c,), lambda c: ("yo", c))
                    self.dma(self.yout[s].rearrange("(c p) t -> p c t", p=128)[:, :, tok], yo[:, :, :],
                             [("yo", c) for c in range(8)], [("yout", s, j)])

            stage_a1(0)
            stage_a2(0)
            prefetch_up(0)
            if NT > 1:
                stage_a1(1)
                load_x(1)
            stage_up(0)
            for j in range(1, NT):
                prefetch_down(j - 1)
                stage_a2(j)
                prefetch_up(j)
                if j + 1 < NT:
                    stage_a1(j + 1)
                stage_down(j - 1)
                if j + 1 < NT:
                    load_x(j + 1)
                stage_up(j)
            prefetch_down(NT - 1)
            stage_down(NT - 1)
            self.flush()


def _rope_tables():
    theta = np.float32(500000.0)
    inv = (theta ** (-np.arange(0, 16, 2, dtype=np.float32) / np.float32(16))).astype(np.float32)
    pos = np.arange(S, dtype=np.float32)
    ang = (pos[:, None] * inv[None, :]).astype(np.float32)
    cos = np.cos(ang).astype(np.float32)
    sin = np.sin(ang).astype(np.float32)
    C = np.ones((64, S), np.float32)
    Sg = np.zeros((64, S), np.float32)
    C[0:8] = cos.T
    C[8:16] = cos.T
    Sg[0:8] = -sin.T
    Sg[8:16] = sin.T
    C = np.concatenate([C, C], 0)
    Sg = np.concatenate([Sg, Sg], 0)
    out = np.zeros((3, 128, 2, S), np.float32)
    for g, d in enumerate(DILS):
        L = S // d
        tp = np.arange(S)
        nat = (tp % L) * d + tp // L
        out[g, :, 0, :] = C[:, nat]
        out[g, :, 1, :] = Sg[:, nat]
    return out


def _masks():
    b = np.arange(128)[:, None]
    a = np.arange(128)[None, :]
    m = np.zeros((128, 6, 128), np.float32)
    m[:, 4, :] = (a == b)
    for mcol in range(128):
        i = mcol % 64
        if i < 8:
            m[mcol + 8, 5, mcol] = 1.0
        elif i < 16:
            m[mcol - 8, 5, mcol] = 1.0
    m[:, 0, :] = (a <= b)
    m[:, 1, :] = (b <= a)
    m[:, 2, :] = (b < 64) & (a - b <= 64)
    m[:, 3, :] = (b < 64) & (b <= a)
    return m


def _partner(cols):
    p = np.array(cols).copy()
    p[0:8] = cols[8:16]
    p[8:16] = cols[0:8]
    return p


def _prep_weights(inp):
    li_e = []
    for i in range(2):
        w = inp["w_in_even"][i]
        cols = []
        pcols = []
        for m in range(4):
            for h in (m, 4 + m):
                hc = np.arange(h * 64, h * 64 + 64)
                cols.append(hc)
                pcols.append(_partner(hc))
        kc = []
        kpc = []
        for kh in range(2):
            hc = np.arange(512 + kh * 64, 512 + kh * 64 + 64)
            kc.append(hc)
            kpc.append(_partner(hc))
        order = np.concatenate(cols + pcols + kc + kpc + [np.arange(768, 1280)] + [np.arange(640, 768)])
        li_e.append(w[:, order])
    w_in_e = np.ascontiguousarray(np.stack(li_e))
    rows = []
    for m in range(4):
        rows += [np.arange(m * 64, m * 64 + 64), np.arange((4 + m) * 64, (4 + m) * 64 + 64)]
    rows.append(np.arange(512, 1024))
    rows = np.concatenate(rows)
    w_out_e = np.ascontiguousarray(inp["w_out_even"][:, rows, :])
    li_o = []
    for i in range(2):
        w = inp["w_in_odd"][i]
        allc = []
        for g in range(3):
            cols = []
            pcols = []
            for c in range(2):
                for gm in range(2):
                    for kv in (2 * c, 2 * c + 1):
                        base = ((g * 4 + kv) * 2 + gm) * 64
                        hc = np.arange(base, base + 64)
                        cols.append(hc)
                        pcols.append(_partner(hc))
            allc += cols + pcols
        kc = []
        kpc = []
        for kh in range(4):
            hc = np.arange(1536 + kh * 64, 1536 + kh * 64 + 64)
            kc.append(hc)
            kpc.append(_partner(hc))
        order = np.concatenate(allc + kc + kpc + [np.arange(1792, 2048)])
        li_o.append(w[:, order])
    w_in_o = np.ascontiguousarray(np.stack(li_o))
    rows = []
    for c in range(2):
        for gm in range(2):
            for kv in (2 * c, 2 * c + 1):
                base = (kv * 2 + gm) * 64
                rows.append(np.arange(base, base + 64))
    rows = np.concatenate(rows)
    w_out_o = np.ascontiguousarray(inp["w_out_odd"][:, rows, :])
    return w_in_e, w_out_e, w_in_o, w_out_o


def _prm(inp):
    prm = np.zeros((128, 96), np.float32)
    for l in range(4):
        prm[:, l * 8:(l + 1) * 8] = inp["norm_mix"][l].reshape(8, 128).T
        prm[:, 32 + l * 8:32 + (l + 1) * 8] = inp["norm_mlp"][l].reshape(8, 128).T
    prm[:, 64:72] = inp["norm_final"].reshape(8, 128).T
    for i in range(2):
        prm[:, 72 + i * 4:72 + (i + 1) * 4] = inp["pool_scale"][i].reshape(4, 128).T
        prm[:, 80 + i * 8:80 + (i + 1) * 8] = inp["sink_logits"][i][None, :]
    return prm


_CACHE = {}


def make_in_maps(inp, ncores, nseq):
    inp = {k: np.asarray(v) for k, v in inp.items()}
    w_in_e, w_out_e, w_in_o, w_out_o = _prep_weights(inp)
    shared = {
        "prm": _prm(inp), "rope": _rope_tables(), "masks": _masks(),
        "w_in_e": w_in_e, "w_out_e": w_out_e, "w_pool": np.ascontiguousarray(inp["w_pool"]),
        "w_in_o": w_in_o, "w_out_o": w_out_o,
        "w_up": np.ascontiguousarray(inp["w_up"]), "w_down": np.ascontiguousarray(inp["w_down"]),
    }
    x = inp["x"]
    maps = []
    for c in range(ncores):
        xT = np.ascontiguousarray(np.transpose(x[c * nseq:(c + 1) * nseq], (0, 2, 1)))
        m = dict(shared)
        m["xT"] = xT
        maps.append(m)
    return maps


def kernel(**inputs):
    ncores = 8
    b = Builder(DEPTH, NSEQ)
    nc = b.build()
    maps = make_in_maps(inputs, ncores, NSEQ)
    res = run_bass_kernel_spmd(nc, maps, core_ids=list(range(ncores)))
    outs = [np.transpose(r["yT"], (0, 2, 1)) for r in res.results]
    return np.ascontiguousarray(np.concatenate(outs, axis=0)).astype(np.float32)
```
